# Optimizing a Trainium2 kernel written in Bass

```python
import jax, jax.numpy as jnp
from jax import lax
import numpy as np

D_MODEL = 1024
BATCH = 4
SEQ = 8192
DEPTH = 2

N_HEADS_A = 8
HEAD_DIM_A = 64
KV_LATENT = 256
IDX_HEADS = 8
IDX_DIM = 64
TOPK_MAX = 256
Q_BLOCK = 128
SSM_D_INNER = 1024
SSM_HEADS = 16
SSM_HEAD_DIM = SSM_D_INNER // SSM_HEADS
SSM_GROUPS = 2
SSM_STATE = 128
SSM_CONV = 4
SSM_CHUNK = 128
SC_WIDTH = D_MODEL
SC_CONV = 3
D_FF = -(-8 * D_MODEL // (3 * 256)) * 256
EPS = 1e-6

A_SIZES = (N_HEADS_A * HEAD_DIM_A, KV_LATENT, IDX_HEADS * IDX_DIM, IDX_DIM, IDX_HEADS)
B_SIZES = (SSM_D_INNER, SSM_D_INNER + 2 * SSM_GROUPS * SSM_STATE, SSM_HEADS)
D_IN_EVEN = sum(A_SIZES) + sum(B_SIZES)
D_OUT_EVEN = N_HEADS_A * HEAD_DIM_A + SSM_D_INNER

kernel_name = "hybrid_dsa_ssd_shortconv_trunk"


def _rmsnorm(x, w):
    xf = x.astype(jnp.float32)
    y = xf * lax.rsqrt(jnp.mean(xf * xf, axis=-1, keepdims=True) + EPS)
    return y.astype(x.dtype) * w


def _split(h, sizes):
    cuts = [int(c) for c in np.cumsum(sizes)[:-1]]
    return jnp.split(h, cuts, axis=-1)


def _causal_dwconv(x, w):
    K = w.shape[0]
    return lax.conv_general_dilated(
        x, w[:, None, :], window_strides=(1,), padding=[(K - 1, 0)],
        dimension_numbers=("NWC", "WIO", "NWC"), feature_group_count=x.shape[-1])


def _dsa_attention(q, ckv, iq, ik, iw, w_uk, w_uv):
    b, S, _ = q.shape
    topk = min(TOPK_MAX, S // 4)
    nb = S // Q_BLOCK
    qb = jnp.moveaxis(q.reshape(b, nb, Q_BLOCK, N_HEADS_A, HEAD_DIM_A), 1, 0)
    iqb = jnp.moveaxis(iq.reshape(b, nb, Q_BLOCK, IDX_HEADS, IDX_DIM), 1, 0)
    iwb = jnp.moveaxis(iw.reshape(b, nb, Q_BLOCK, IDX_HEADS), 1, 0) * (IDX_HEADS ** -0.5)
    t0s = jnp.arange(nb, dtype=jnp.int32) * Q_BLOCK
    s_pos = jnp.arange(S, dtype=jnp.int32)

    def block(args):
        q_blk, iq_blk, iw_blk, t0 = args
        t_pos = t0 + jnp.arange(Q_BLOCK, dtype=jnp.int32)
        causal = s_pos[None, :] <= t_pos[:, None]
        il = jnp.einsum('bthd,bsd->bths', iq_blk, ik) * (IDX_DIM ** -0.5)
        score = jnp.einsum('bths,bth->bts', jax.nn.relu(il), iw_blk)
        score = jnp.where(causal[None], score, -jnp.inf)
        _, sel = lax.top_k(score, topk)
        c_sel = jax.vmap(lambda c, i: c[i])(ckv, sel)
        q_lat = jnp.einsum('bthd,hdc->bthc', q_blk, w_uk)
        logits = jnp.einsum('bthc,btkc->bthk', q_lat, c_sel).astype(jnp.float32) * (HEAD_DIM_A ** -0.5)
        valid = sel <= t_pos[None, :, None]
        logits = jnp.where(valid[:, :, None, :], logits, -jnp.inf)
        p = jax.nn.softmax(logits, axis=-1).astype(c_sel.dtype)
        o_lat = jnp.einsum('bthk,btkc->bthc', p, c_sel)
        o = jnp.einsum('bthc,hcd->bthd', o_lat, w_uv)
        return o.reshape(b, Q_BLOCK, N_HEADS_A * HEAD_DIM_A)

    out = lax.map(block, (qb, iqb, iwb, t0s))
    return jnp.moveaxis(out, 0, 1).reshape(b, S, N_HEADS_A * HEAD_DIM_A)


def _ssd_chunked(xs, dt, A, Bm, Cm):
    b, S, h, p = xs.shape
    g, n = Bm.shape[2], Bm.shape[3]
    r = h // g
    L = SSM_CHUNK
    c = S // L
    a = (dt * A).reshape(b, c, L, g, r)
    xdt = (xs * dt[..., None]).reshape(b, c, L, g, r, p)
    Bc = Bm.reshape(b, c, L, g, n)
    Cc = Cm.reshape(b, c, L, g, n)
    a_cs = jnp.cumsum(a, axis=2)
    tri = jnp.tril(jnp.ones((L, L), dtype=bool))
    seg = a_cs[:, :, :, None] - a_cs[:, :, None, :]
    decay = jnp.exp(jnp.where(tri[:, :, None, None], seg, -jnp.inf))
    cb = jnp.einsum('bclgn,bcsgn->bclsg', Cc, Bc)
    y_diag = jnp.einsum('bclsgr,bcsgrp->bclgrp', cb[..., None] * decay, xdt)
    decay_end = jnp.exp(a_cs[:, :, -1:] - a_cs)
    states = jnp.einsum('bclgn,bclgr,bclgrp->bcgrpn', Bc, decay_end, xdt)
    chunk_decay = jnp.exp(a_cs[:, :, -1])

    def step(hs, inp):
        dec, st = inp
        return hs * dec[..., None, None] + st, hs

    _, prev = lax.scan(step, jnp.zeros_like(states[:, 0]),
                       (jnp.moveaxis(chunk_decay, 1, 0), jnp.moveaxis(states, 1, 0)))
    prev = jnp.moveaxis(prev, 0, 1)
    y_off = jnp.einsum('bclgn,bcgrpn,bclgr->bclgrp', Cc, prev, jnp.exp(a_cs))
    return (y_diag + y_off).reshape(b, S, h, p)


def _mamba2(z, xbc, dt, conv_w, conv_b, dt_bias, A_log, D, norm_w):
    b, S, _ = z.shape
    xbc = jax.nn.silu(_causal_dwconv(xbc, conv_w) + conv_b)
    xs, Bm, Cm = _split(xbc, (SSM_D_INNER, SSM_GROUPS * SSM_STATE, SSM_GROUPS * SSM_STATE))
    xs = xs.reshape(b, S, SSM_HEADS, SSM_HEAD_DIM)
    Bm = Bm.reshape(b, S, SSM_GROUPS, SSM_STATE)
    Cm = Cm.reshape(b, S, SSM_GROUPS, SSM_STATE)
    dt = jax.nn.softplus(dt + dt_bias)
    A = -jnp.exp(A_log)
    y = _ssd_chunked(xs, dt, A, Bm, Cm) + xs * D[:, None]
    y = y.reshape(b, S, SSM_D_INNER) * jax.nn.silu(z)
    y = _rmsnorm(y.reshape(b, S, SSM_GROUPS, SSM_D_INNER // SSM_GROUPS),
                 norm_w.reshape(SSM_GROUPS, SSM_D_INNER // SSM_GROUPS))
    return y.reshape(b, S, SSM_D_INNER)


def _parallel_mixer_layer(x, attn_norm, in_w, kv_norm, w_uk, w_uv, conv_w, conv_b,
                          dt_bias, A_log, D, ssm_norm, out_w):
    h = _rmsnorm(x, attn_norm) @ in_w
    q, ckv, iq, ik, iw, z, xbc, dt = _split(h, A_SIZES + B_SIZES)
    y_a = _dsa_attention(q, _rmsnorm(ckv, kv_norm), iq, ik, iw, w_uk, w_uv)
    y_b = _mamba2(z, xbc, dt, conv_w, conv_b, dt_bias, A_log, D, ssm_norm)
    return x + jnp.concatenate([y_a, y_b], axis=-1) @ out_w


def _short_conv_layer(x, norm_w, in_w, conv_w, out_w):
    h = _rmsnorm(x, norm_w) @ in_w
    gate_b, gate_c, v = _split(h, (SC_WIDTH, SC_WIDTH, SC_WIDTH))
    y = gate_b * _causal_dwconv(gate_c * v, conv_w)
    return x + y @ out_w


def _swiglu_ffn(x, norm_w, w_gate, w_up, w_down):
    h = _rmsnorm(x, norm_w)
    return x + (jax.nn.silu(h @ w_gate) * (h @ w_up)) @ w_down


def setup_inputs(seed: int = 0) -> dict:
    key = jax.random.key(seed)
    ks = jax.random.split(key, 32)

    def nrm(k, shape, fan_in):
        return jax.random.normal(k, shape, jnp.float32) * (fan_in ** -0.5)

    def gain(k, n):
        return 1.0 + 0.02 * jax.random.normal(k, (n,), jnp.float32)

    dt0 = jnp.exp(jax.random.uniform(ks[10], (SSM_HEADS,), jnp.float32, np.log(1e-3), np.log(1e-1)))
    return {
        "x": jax.random.normal(ks[0], (BATCH, SEQ, D_MODEL), jnp.float32),
        "l0_attn_norm": gain(ks[1], D_MODEL),
        "l0_in_w": nrm(ks[2], (D_MODEL, D_IN_EVEN), D_MODEL),
        "l0_kv_norm": gain(ks[3], KV_LATENT),
        "l0_w_uk": nrm(ks[4], (N_HEADS_A, HEAD_DIM_A, KV_LATENT), KV_LATENT),
        "l0_w_uv": nrm(ks[5], (N_HEADS_A, KV_LATENT, HEAD_DIM_A), KV_LATENT),
        "l0_conv_w": nrm(ks[6], (SSM_CONV, SSM_D_INNER + 2 * SSM_GROUPS * SSM_STATE), SSM_CONV),
        "l0_conv_b": 0.02 * jax.random.normal(ks[7], (SSM_D_INNER + 2 * SSM_GROUPS * SSM_STATE,), jnp.float32),
        "l0_dt_bias": dt0 + jnp.log(-jnp.expm1(-dt0)),
        "l0_A_log": jnp.log(jax.random.uniform(ks[8], (SSM_HEADS,), jnp.float32, 1.0, 16.0)),
        "l0_D": 1.0 + 0.1 * jax.random.normal(ks[9], (SSM_HEADS,), jnp.float32),
        "l0_ssm_norm": gain(ks[11], SSM_D_INNER),
        "l0_out_w": nrm(ks[12], (D_OUT_EVEN, D_MODEL), D_OUT_EVEN),
        "l0_ffn_norm": gain(ks[13], D_MODEL),
        "l0_w_gate": nrm(ks[14], (D_MODEL, D_FF), D_MODEL),
        "l0_w_up": nrm(ks[15], (D_MODEL, D_FF), D_MODEL),
        "l0_w_down": nrm(ks[16], (D_FF, D_MODEL), D_FF),
        "l1_conv_norm": gain(ks[17], D_MODEL),
        "l1_in_w": nrm(ks[18], (D_MODEL, 3 * SC_WIDTH), D_MODEL),
        "l1_conv_w": nrm(ks[19], (SC_CONV, SC_WIDTH), SC_CONV),
        "l1_out_w": nrm(ks[20], (SC_WIDTH, D_MODEL), SC_WIDTH),
        "l1_ffn_norm": gain(ks[21], D_MODEL),
        "l1_w_gate": nrm(ks[22], (D_MODEL, D_FF), D_MODEL),
        "l1_w_up": nrm(ks[23], (D_MODEL, D_FF), D_MODEL),
        "l1_w_down": nrm(ks[24], (D_FF, D_MODEL), D_FF),
        "final_norm": gain(ks[25], D_MODEL),
    }


def reference(x, l0_attn_norm, l0_in_w, l0_kv_norm, l0_w_uk, l0_w_uv, l0_conv_w, l0_conv_b,
              l0_dt_bias, l0_A_log, l0_D, l0_ssm_norm, l0_out_w, l0_ffn_norm, l0_w_gate,
              l0_w_up, l0_w_down, l1_conv_norm, l1_in_w, l1_conv_w, l1_out_w, l1_ffn_norm,
              l1_w_gate, l1_w_up, l1_w_down, final_norm):
    layers = (
        ((l0_attn_norm, l0_in_w, l0_kv_norm, l0_w_uk, l0_w_uv, l0_conv_w, l0_conv_b,
          l0_dt_bias, l0_A_log, l0_D, l0_ssm_norm, l0_out_w),
         (l0_ffn_norm, l0_w_gate, l0_w_up, l0_w_down)),
        ((l1_conv_norm, l1_in_w, l1_conv_w, l1_out_w),
         (l1_ffn_norm, l1_w_gate, l1_w_up, l1_w_down)),
    )
    for i in range(DEPTH):
        mix_p, ffn_p = layers[i]
        if i % 2 == 0:
            x = _parallel_mixer_layer(x, *mix_p)
        else:
            x = _short_conv_layer(x, *mix_p)
        x = _swiglu_ffn(x, *ffn_p)
    return _rmsnorm(x, final_norm)
```

```python
import os
import numpy as np
import concourse.bass as bass
import concourse.mybir as mybir
from concourse.bass_utils import run_bass_kernel_spmd
from contextlib import ExitStack

F32 = mybir.dt.float32
BF16 = mybir.dt.bfloat16
AF = mybir.ActivationFunctionType
ALU = mybir.AluOpType
AX = mybir.AxisListType

ENGS = ("sync", "scalar", "vector", "gpsimd", "tensor")
NDMASEM = 8

D = 1024
S = 8192
HALF = 4096
EXT = 4224
EXT0 = S - EXT
TW = 384
NT = EXT // TW
DFF = 2816
NFF = DFF // 128
NEG = -1.0e30


class Buf:
    __slots__ = ("name", "w", "r")

    def __init__(self, name):
        self.name = name
        self.w = {}
        self.r = {}


class T:
    def __init__(self, t, name):
        self.t = t
        self.b = Buf(name)

    def __getitem__(self, idx):
        return self.t[idx]

    def rb(self, c0, c1):
        sub = getattr(self, "sub", None)
        if not sub:
            return [self]
        return [sub[j] for j in range(c0 // 512, (c1 - 1) // 512 + 1) if j in sub] or [self]


class _B:
    def __init__(self, name):
        self.b = Buf(name)


class Prog:
    def __init__(self, nc, es):
        self.nc = nc
        self.es = es
        self.q = {e: [] for e in ENGS}
        self.sems = {}
        self.cnt = {}
        self.waited = {e: {} for e in ENGS}
        self.dma_k = {e: 0 for e in ENGS}
        for e in ("scalar", "vector", "gpsimd", "tensor"):
            self._mksem("E_" + e)
        self.nwaits = 0
        self.nops = 0

    def _mksem(self, key):
        self.sems[key] = self.es.enter_context(self.nc.semaphore(key))
        self.cnt[key] = 0

    def _need(self, eng, toks, key, val):
        if key == "E_tensor" and eng == "tensor":
            return
        if self.waited[eng].get(key, 0) >= val:
            return
        if toks.get(key, 0) < val:
            toks[key] = val

    def op(self, eng, fn, reads=(), writes=(), dma=False, partial=False):
        toks = {}
        for b in reads:
            for k, v in b.w.items():
                self._need(eng, toks, k, v)
        for b in writes:
            for k, v in b.w.items():
                self._need(eng, toks, k, v)
            for k, v in b.r.items():
                self._need(eng, toks, k, v)
        if dma:
            k = self.dma_k[eng]
            self.dma_k[eng] += 1
            key = "D_%s_%d" % (eng, k % NDMASEM)
            if key not in self.sems:
                self._mksem(key)
            prev = self.cnt[key]
            if prev > 0:
                self._need(eng, toks, key, prev)
            self.cnt[key] += 16
            inc = 16
        else:
            key = "E_" + eng
            self.cnt[key] += 1
            inc = 1
        val = self.cnt[key]
        for k2, v2 in toks.items():
            self.waited[eng][k2] = v2
        self.nwaits += len(toks)
        self.nops += 1
        self.q[eng].append((list(toks.items()), fn, key, inc))
        for b in reads:
            b.r[key] = val
        for b in writes:
            if partial:
                b.w[key] = val
            else:
                b.w = {key: val}
                b.r = {}
        return (key, val)

    def barrier(self):
        for e in ENGS:
            toks = {}
            for k, v in self.cnt.items():
                if v > 0:
                    self._need(e, toks, k, v)
            for k2, v2 in toks.items():
                self.waited[e][k2] = v2
            if toks:
                self.q[e].append((list(toks.items()), None, None, 0))

    def emit(self):
        nc = self.nc
        with nc.Block() as block:
            for e in ENGS:
                ops = self.q[e]

                def body(engine, ops=ops):
                    for waits, fn, key, inc in ops:
                        for k, v in waits:
                            engine.wait_ge(self.sems[k], v)
                        if fn is not None:
                            ins = fn(engine)
                            ins.then_inc(self.sems[key], inc)

                getattr(block, e)(body)


class I:
    def __init__(self, name, *a, **kw):
        self.name = name
        self.a = a
        self.kw = kw

    def __call__(self, e):
        return getattr(e, self.name)(*self.a, **self.kw)


class KB:
    def __init__(self, nc, P):
        self.nc = nc
        self.P = P
        self.dq = 0

    def sb(self, es, name, shape, dt):
        self.dq += 1
        name = "sb%d_%s" % (self.dq, name)
        return T(es.enter_context(self.nc.sbuf_tensor(name, list(shape), dt)), name)

    def ps(self, es, name, shape=(128, 512), dt=F32):
        self.dq += 1
        name = "pp%d_%s" % (self.dq, name)
        return T(es.enter_context(self.nc.psum_tensor(name, list(shape), dt)), name)

    def _op(self, eng, fn, r, w, **kw):
        return self.P.op(eng, fn, [x.b for x in r], [x.b for x in w], **kw)

    def V(self, fn, r, w, **kw):
        return self._op("vector", fn, r, w, **kw)

    def A(self, fn, r, w, **kw):
        return self._op("scalar", fn, r, w, **kw)

    def G(self, fn, r, w, **kw):
        return self._op("gpsimd", fn, r, w, **kw)

    def PE(self, fn, r, w, **kw):
        return self._op("tensor", fn, r, w, **kw)

    def DMA(self, out, in_, r, w, eng="sync", partial=True):
        return self._op(eng, I("dma_start", out=out, in_=in_), r, w, dma=True, partial=partial)


class Shared:
    pass


def load_weight(kb, sh, dst, src, kchunks, ncols, gain=None, dcol0=0, act_share=False):
    srcv = src.rearrange("(k p) n -> p k n", p=128)
    if not hasattr(dst, "sub"):
        dst.sub = {}
    c = 0
    while c < ncols:
        cw = min(512 - (dcol0 + c) % 512, ncols - c)
        j = (dcol0 + c) // 512
        if j not in dst.sub:
            dst.sub[j] = _B("w%d" % j)
        sb_ = dst.sub[j]
        for k0 in range(0, kchunks, 4):
            nk = min(4, kchunks - k0)
            st = sh.stage[sh.stage_i % 2]
            sh.stage_i += 1
            stv = st[:, :nk * cw].rearrange("p (k n) -> p k n", k=nk)
            kb.DMA(stv, srcv[:, k0:k0 + nk, c:c + cw], [], [st], partial=False)
            for kk in range(nk):
                k = k0 + kk
                o = dst[:, k, dcol0 + c:dcol0 + c + cw]
                if gain is None and act_share and kk % 2 == 1:
                    kb.A(I("activation", out=o, in_=stv[:, kk, :], func=AF.Copy), [st], [sb_], partial=True)
                elif gain is None:
                    kb.G(I("tensor_scalar", out=o, in0=stv[:, kk, :], scalar1=1.0, scalar2=None, op0=ALU.mult), [st], [sb_], partial=True)
                else:
                    kb.G(I("tensor_scalar", out=o, in0=stv[:, kk, :], scalar1=gain[:, k:k + 1], scalar2=1.0, op0=ALU.mult, op1=ALU.mult),
                         [st, gain], [sb_], partial=True)
        c += cw


def rmsnorm_fm(kb, sh, xf, W, xn, nk=8, dim=1024.0, psb=None):
    sq = sh.sq
    kb.A(I("activation", out=sq[:, :nk, :W], in_=xf[:, :nk, :W], func=AF.Square), [xf], [sq])
    ps = psb if psb is not None else sh.next_ps()
    for k in range(nk):
        kb.PE(I("matmul", ps[:, :W], lhsT=sh.ones[:, :], rhs=sq[:, k, :W],
                                      start=(k == 0), stop=(k == nk - 1)), [sh.ones, sq], [ps])
    rs = sh.rstd[sh.rstd_i % 2]
    sh.rstd_i += 1
    kb.A(I("activation", out=rs[:, :W], in_=ps[:, :W], func=AF.Sqrt, scale=1.0 / dim, bias=sh.eps[:, 0:1]),
         [ps, sh.eps], [rs])
    kb.V(I("reciprocal", out=rs[:, :W], in_=rs[:, :W]), [rs], [rs])
    if xn is not None:
        for k in range(nk):
            kb.V(I("tensor_tensor", out=xn[:, k, :W], in0=xf[:, k, :W], in1=rs[:, :W], op=ALU.mult),
                 [xf, rs], [xn], partial=(k > 0))
    return rs


def proj(kb, ps, wt, col0, xn, W, nk=8, M=128, k0=0, start=True, stop=True, xk0=0):
    for k in range(nk):
        kb.PE(I("matmul", ps[:M, :W], lhsT=wt[:, k0 + k, col0:col0 + M], rhs=xn[:, xk0 + k, :W],
                                      start=(start and k == 0), stop=(stop and k == nk - 1)), wt.rb(col0, col0 + M) + [xn], [ps])


def build_program(debug=None):
    nc = bass.Bass("TRN2", target_bir_lowering=False)
    dbg = {}

    def din(name, shape, dt=F32):
        return nc.dram_tensor(name, list(shape), dt, kind="ExternalInput").ap()

    def dscr(name, shape, dt):
        return nc.dram_tensor(name, list(shape), dt, kind="Internal").ap()

    xT = din("xT", [D, S])
    consts = din("consts", [128, 1024])
    flags = din("flags", [128, 2])
    vecs = din("vecs", [128, 64])
    l0_in_w = din("l0_in_w", [D, 3928])
    l0_out_w = din("l0_out_w", [1536, D])
    wukT_d = din("wukT", [256, 512])
    wuv_d = din("wuv", [256, 512])
    vecs2 = din("vecs2", [128, 128])
    l1_in_w = din("l1_in_w", [D, 3072])
    l1_out_w = din("l1_out_w", [D, D])
    ffn_w = [(din("l0_w_gate", [D, DFF]), din("l0_w_up", [D, DFF]), din("l0_w_down", [DFF, D])),
             (din("l1_w_gate", [D, DFF]), din("l1_w_up", [D, DFF]), din("l1_w_down", [DFF, D]))]
    yT = nc.dram_tensor("yT", [D, HALF], F32, kind="ExternalOutput").ap()

    xa_d = dscr("xa_d", [D, EXT], F32)
    xb_d = dscr("xb_d", [D, EXT], F32)
    xc_d = dscr("xc_d", [D, EXT], F32)
    h_d = dscr("h_d", [DFF, EXT], BF16)
    DBG = (debug in ("p1a", "p1b", "p2"))

    def dscr2(name, shape, dt):
        if DBG:
            return nc.dram_tensor(name, list(shape), dt, kind="ExternalOutput").ap()
        return dscr(name, shape, dt)
    kT_d = dscr2("kT_d", [512, S], BF16)
    V_d = dscr2("V_d", [8, 128, 64, 65], BF16)
    ikT_d = dscr2("ikT_d", [128, S], BF16)
    qT_d = dscr2("qT_d", [512, EXT], BF16)
    iqT_d = dscr2("iqT_d", [512, EXT], BF16)
    iw_d = dscr2("iw_d", [33, 128, 16], F32)
    mx_d = dscr2("mx_d", [128, 16], F32)
    ybT_d = dscr2("ybT_d", [D, EXT], BF16)
    yaT_d = dscr2("yaT_d", [512, EXT], BF16)

    VC = dict(l0_attn_norm=0, l0_ffn_norm=8, l1_conv_norm=16, l1_ffn_norm=24, final_norm=32,
              l1_conv_w=40)

    with ExitStack() as es:
        P = Prog(nc, es)
        kb = KB(nc, P)
        sh = Shared()
        cst = kb.sb(es, "cst", [128, 1024], F32)
        sh.ident_f = cst
        sh.ones = kb.sb(es, "ones", [128, 128], BF16)
        sh.ident = kb.sb(es, "ident", [128, 128], BF16)
        sh.eps = kb.sb(es, "eps", [128, 1], F32)
        sh.flags = kb.sb(es, "flags", [128, 2], F32)
        sh.vecs = kb.sb(es, "vecs", [128, 64], F32)
        kb.DMA(cst[:, :], consts[:, :], [], [cst], partial=False)
        kb.DMA(sh.flags[:, :], flags[:, :], [], [sh.flags], partial=False)
        kb.DMA(sh.vecs[:, :], vecs[:, :], [], [sh.vecs], partial=False)
        kb.V(I("tensor_copy", out=sh.ones[:, :], in_=cst[:, 384:512]), [cst], [sh.ones])
        kb.V(I("tensor_copy", out=sh.ident[:, :], in_=cst[:, 0:128]), [cst], [sh.ident])
        kb.V(I("memset", sh.eps[:, :], 1e-6), [], [sh.eps])
        pall = es.enter_context(nc.psum_tensor("pall", [128, 8, 512], F32))

        class PB:
            def __init__(self, ap, name):
                self.ap = ap
                self.b = Buf(name)

            def __getitem__(self, idx):
                return self.ap[idx]

            def bf(self):
                return self.ap.bitcast(BF16)

        psums = [PB(pall[:, i, :], "ps%d" % i) for i in range(8)]
        pdbl = [PB(pall[:, 4 + 2 * i:6 + 2 * i, :].rearrange("p a b -> p (a b)"), "pd%d" % i) for i in range(2)]
        sh.ps_i = 0
        sh.nps = 8
        sh.pd_i = 0

        def next_pd():
            p = pdbl[sh.pd_i % 2]
            sh.pd_i += 1
            return p

        def next_ps():
            p = psums[sh.ps_i % sh.nps]
            sh.ps_i += 1
            return p
        sh.next_ps = next_ps

        def gain(name):
            return _GainView(sh.vecs, VC[name])

        class _GainView:
            def __init__(self, t, c):
                self.t = t.t
                self.b = t.b
                self.c = c

            def __getitem__(self, idx):
                p, k = idx
                if isinstance(k, slice):
                    k = slice(k.start + self.c, k.stop + self.c)
                else:
                    k = k + self.c
                return self.t[p, k]

        def common_tiles(pes):
            sh.stage = [kb.sb(pes, "stage%d" % i, [128, 2048], F32) for i in range(2)]
            sh.stage_i = 0
            sh.sq = kb.sb(pes, "sq", [128, 8, 512], BF16)
            sh.rstd = [kb.sb(pes, "rstd%d" % i, [128, 512], F32) for i in range(2)]
            sh.rstd_i = 0

        def xview(d, k=8):
            return d.rearrange("(k p) t -> p k t", p=128)

        def phase_ffn_a(src_d, wg_d, wu_d, gname, pre=None):
            with ExitStack() as pes:
                common_tiles(pes)
                sh.nps = 8
                wg = kb.sb(pes, "wg", [128, 8, DFF], BF16)
                wu = kb.sb(pes, "wu", [128, 8, DFF], BF16)
                g = gain(gname)
                xf = [kb.sb(pes, "xf%d" % i, [128, 8, TW], F32) for i in range(2)]
                xn = [kb.sb(pes, "xn%d" % i, [128, 8, TW], BF16) for i in range(2)]
                hT = [kb.sb(pes, "hT%d" % i, [128, NFF, TW], BF16) for i in range(2)]
                sg = [kb.sb(pes, "sg%d" % i, [128, TW], BF16) for i in range(3)]
                sv = xview(src_d)
                hv = h_d.rearrange("(k p) t -> p k t", p=128)
                sgi = [0]

                def a_pre(i):
                    t0 = i * TW
                    x_, n_ = xf[i % 2], xn[i % 2]
                    kb.DMA(x_[:, :, :], sv[:, :, t0:t0 + TW], [], [x_], partial=False)
                    rmsnorm_fm(kb, sh, x_, TW, n_)

                def a_step(i, n):
                    n_, h_ = xn[i % 2], hT[i % 2]
                    pg, pu = next_ps(), next_ps()
                    proj(kb, pg, wg, n * 128, n_, TW)
                    proj(kb, pu, wu, n * 128, n_, TW)
                    s_ = sg[sgi[0] % 3]
                    sgi[0] += 1
                    kb.A(I("activation", out=s_[:, :], in_=pg[:, :TW], func=AF.Silu), [pg], [s_])
                    kb.V(I("tensor_tensor",
                        out=h_[:, n, :], in0=pu[:, :TW], in1=s_[:, :], op=ALU.mult), [pu, s_], [h_], partial=(n > 0))

                def a_post(i):
                    t0 = i * TW
                    h_ = hT[i % 2]
                    kb.DMA(hv[:, :, t0:t0 + TW], h_[:, :, :], [h_], [], eng="gpsimd")

                a_pre(0)
                a_pre(1)
                for c_ in range(0, DFF, 512):
                    cw_ = min(512, DFF - c_)
                    load_weight(kb, sh, wg, wg_d[:, c_:c_ + cw_], 8, cw_, g, c_)
                    load_weight(kb, sh, wu, wu_d[:, c_:c_ + cw_], 8, cw_, g, c_)
                for n in range(NFF):
                    a_step(0, n)
                    a_step(1, n)
                a_post(0)
                a_post(1)
                for i in range(2, NT):
                    a_pre(i)
                    for n in range(NFF):
                        a_step(i, n)
                    a_post(i)
                P.barrier()

        def phase_ffn_b(src_d, wd_d, dst_d=None, final_gain=None):
            with ExitStack() as pes:
                common_tiles(pes)
                sh.nps = 8
                wd = kb.sb(pes, "wd", [128, NFF, D], BF16)
                xf = [kb.sb(pes, "xf%d" % i, [128, 8, TW], F32) for i in range(2)]
                xo = [kb.sb(pes, "xo%d" % i, [128, 8, TW], F32) for i in range(2)]
                hT = [kb.sb(pes, "hT%d" % i, [128, NFF, TW], BF16) for i in range(2)]
                sv = xview(src_d)
                hv = h_d.rearrange("(k p) t -> p k t", p=128)
                if final_gain is not None:
                    fg = gain(final_gain)
                    yo = [kb.sb(pes, "yo%d" % i, [128, 8, TW], F32) for i in range(2)]
                    yv = xview(yT)
                else:
                    dv = xview(dst_d)
                def b_pre(i):
                    t0 = i * TW
                    x_, h_ = xf[i % 2], hT[i % 2]
                    kb.DMA(x_[:, :, :], sv[:, :, t0:t0 + TW], [], [x_], partial=False)
                    kb.DMA(h_[:, :, :], hv[:, :, t0:t0 + TW], [], [h_], partial=False)

                def b_step(i, n):
                    x_, o_, h_ = xf[i % 2], xo[i % 2], hT[i % 2]
                    ps = next_ps()
                    proj(kb, ps, wd, n * 128, h_, TW, nk=NFF)
                    kb.V(I("tensor_tensor",
                        out=o_[:, n, :], in0=ps[:, :TW], in1=x_[:, n, :], op=ALU.add), [ps, x_], [o_], partial=(n > 0))

                b_pre(0)
                b_pre(1)
                load_weight(kb, sh, wd, wd_d, NFF, D, None, act_share=True)
                for n in range(8):
                    b_step(0, n)
                    b_step(1, n)
                for i in range(NT):
                    t0 = i * TW
                    x_, o_, h_ = xf[i % 2], xo[i % 2], hT[i % 2]
                    if i >= 2:
                        b_pre(i)
                        for n in range(8):
                            b_step(i, n)
                    if final_gain is None:
                        kb.DMA(dv[:, :, t0:t0 + TW], o_[:, :, :], [o_], [], eng="gpsimd")
                    else:
                        rs = rmsnorm_fm(kb, sh, o_, TW, None)
                        y_ = yo[i % 2]
                        for k in range(8):
                            kb.V(I("scalar_tensor_tensor",
                                out=y_[:, k, :], in0=o_[:, k, :], scalar=fg[:, k:k + 1], in1=rs[:, :TW],
                                op0=ALU.mult, op1=ALU.mult), [o_, rs, sh.vecs], [y_], partial=(k > 0))
                        c0 = 128 if i == 0 else 0
                        kb.DMA(yv[:, :, t0 + c0 - 128:t0 + TW - 128], y_[:, :, c0:TW], [y_], [sh.outbuf], eng="gpsimd")
                P.barrier()

        def phase_l1_mixer(src_d, dst_d):
            with ExitStack() as pes:
                common_tiles(pes)
                sh.nps = 8
                wi = kb.sb(pes, "wi", [128, 8, 3072], BF16)
                wo = kb.sb(pes, "wo", [128, 8, D], BF16)
                dg = kb.sb(pes, "dg", [128, 8, 3, 128], BF16)
                cw = VC["l1_conv_w"]
                for n in range(8):
                    for k in range(3):
                        kb.V(I("tensor_scalar",
                            out=dg[:, n, k, :], in0=cst[:, 0:128], scalar1=sh.vecs[:, cw + n * 3 + k:cw + n * 3 + k + 1],
                            scalar2=None, op0=ALU.mult), [cst, sh.vecs], [dg], partial=True)
                xf = [kb.sb(pes, "xf%d" % i, [128, 8, TW], F32) for i in range(2)]
                xn = [kb.sb(pes, "xn%d" % i, [128, 8, TW], BF16) for i in range(2)]
                xo = [kb.sb(pes, "xo%d" % i, [128, 8, TW], F32) for i in range(2)]
                u = [kb.sb(pes, "u%d" % i, [128, 8, TW + 2], BF16) for i in range(2)]
                yb = [kb.sb(pes, "yb%d" % i, [128, 8, TW], BF16) for i in range(2)]
                tmp = [kb.sb(pes, "tmp%d" % i, [128, TW], F32) for i in range(3)]
                sv = xview(src_d)
                dv = xview(dst_d)
                kb.V(I("memset", u[0][:, :, 0:2], 0.0), [], [u[0]])
                for i in range(2):
                    kb.DMA(xf[i][:, :, :], sv[:, :, i * TW:(i + 1) * TW], [], [xf[i]], partial=False)
                gl1 = gain("l1_conv_norm")
                load_weight(kb, sh, wi, l1_in_w[:, 1024:3072], 8, 2048, gl1, 1024)
                load_weight(kb, sh, wi, l1_in_w[:, 0:1024], 8, 1024, gl1, 0)
                load_weight(kb, sh, wo, l1_out_w, 8, D, None)
                for i in range(NT):
                    t0 = i * TW
                    x_, n_, o_, u_, y_ = xf[i % 2], xn[i % 2], xo[i % 2], u[i % 2], yb[i % 2]
                    up = u[(i + 1) % 2]
                    if i >= 2:
                        kb.DMA(x_[:, :, :], sv[:, :, t0:t0 + TW], [], [x_], partial=False)
                    rmsnorm_fm(kb, sh, x_, TW, n_)
                    if i > 0:
                        kb.V(I("tensor_copy", out=u_[:, :, 0:2], in_=up[:, :, TW:TW + 2]), [up], [u_])
                    for n in range(8):
                        pc, pv = next_ps(), next_ps()
                        proj(kb, pc, wi, 1024 + n * 128, n_, TW)
                        proj(kb, pv, wi, 2048 + n * 128, n_, TW)
                        t_ = tmp[n % 3]
                        kb.A(I("activation", out=t_[:, :], in_=pc[:, :TW], func=AF.Copy), [pc], [t_])
                        kb.V(I("tensor_tensor",
                            out=u_[:, n, 2:TW + 2], in0=pv[:, :TW], in1=t_[:, :], op=ALU.mult), [pv, t_], [u_], partial=True)
                    if i == 0:
                        kb.V(I("tensor_scalar",
                            out=u_[:, :, 2:130], in0=u_[:, :, 2:130], scalar1=sh.flags[:, 0:1], scalar2=None,
                            op0=ALU.mult), [u_, sh.flags], [u_])
                    for n in range(8):
                        pcv, pb = next_ps(), next_ps()
                        for k in range(3):
                            kb.PE(I("matmul",
                                pcv[:, :TW], lhsT=dg[:, n, k, :], rhs=u_[:, n, k:k + TW], start=(k == 0), stop=(k == 2)),
                                [dg, u_], [pcv])
                        proj(kb, pb, wi, n * 128, n_, TW)
                        t_ = tmp[n % 3]
                        kb.A(I("activation", out=t_[:, :], in_=pb[:, :TW], func=AF.Copy), [pb], [t_])
                        kb.V(I("tensor_tensor",
                            out=y_[:, n, :], in0=pcv[:, :TW], in1=t_[:, :], op=ALU.mult), [pcv, t_], [y_], partial=(n > 0))
                    for n in range(8):
                        ps = next_ps()
                        proj(kb, ps, wo, n * 128, y_, TW)
                        kb.V(I("tensor_tensor",
                            out=o_[:, n, :], in0=ps[:, :TW], in1=x_[:, n, :], op=ALU.add), [ps, x_], [o_], partial=(n > 0))
                    kb.DMA(dv[:, :, t0:t0 + TW], o_[:, :, :], [o_], [], eng="gpsimd")
                P.barrier()

        tiles_all = [(0, 128, None)] + [(128 + 384 * i, 384, None) for i in range(10)] + \
                    [(EXT0 + TW * i, TW, TW * i) for i in range(NT)]
        sh.v2 = kb.sb(es, "v2", [128, 128], F32)
        kb.DMA(sh.v2[:, :], vecs2[:, :], [], [sh.v2], partial=False)
        sh.sel = kb.sb(es, "sel", [128, 2, 128], BF16)
        kb.V(I("tensor_copy", out=sh.sel[:, :, :], in_=cst[:, 512:768].rearrange("p (a b) -> p a b", a=2)), [cst], [sh.sel])
        sh.kmax = kb.sb(es, "kmax", [128, 8], F32)
        sh.qmax = kb.sb(es, "qmax", [128, 8], F32)
        kb.V(I("memset", sh.kmax[:, :], 0.0), [], [sh.kmax])
        kb.V(I("memset", sh.qmax[:, :], 0.0), [], [sh.qmax])

        def phase_1a():
            LVL = int(os.environ.get("LVL", "9"))
            with ExitStack() as pes:
                common_tiles(pes)
                sh.nps = 8
                Wa = kb.sb(pes, "Wa", [128, 8, 1416], BF16)
                g = gain("l0_attn_norm")
                wuk = kb.sb(pes, "wuk", [128, 2, 512], BF16)
                wuv = kb.sb(pes, "wuv", [128, 2, 512], BF16)
                xf = [kb.sb(pes, "xf%d" % i, [128, 8, TW], F32) for i in range(2)]
                xn = [kb.sb(pes, "xn%d" % i, [128, 8, TW], BF16) for i in range(2)]
                sqk = kb.sb(pes, "sqk", [128, 2, TW], BF16)
                ckvn = kb.sb(pes, "ckvn", [128, 2, TW], BF16)
                sqq = [kb.sb(pes, "sqq%d" % i, [128, TW], BF16) for i in range(2)]
                kT_sb = [kb.sb(pes, "kT%d" % i, [128, 4, TW], BF16) for i in range(2)]
                qT_sb = [kb.sb(pes, "qT%d" % i, [128, 4, TW], BF16) for i in range(2)]
                iq_sb = [kb.sb(pes, "iq%d" % i, [128, 4, TW], BF16) for i in range(2)]
                V_sb = [kb.sb(pes, "V%d" % i, [128, 3, 8, 65], BF16) for i in range(2)]
                ik_sb = [kb.sb(pes, "ik%d" % i, [128, TW], BF16) for i in range(2)]
                iw_sb = [kb.sb(pes, "iw%d" % i, [128, 3, 16], F32) for i in range(2)]
                mt = kb.sb(pes, "mt", [128, 8], F32)
                for v_ in V_sb:
                    kb.V(I("memset", v_[:, :, :, :], 1.0), [], [v_])
                kTv = kT_d.rearrange("(j p) t -> p j t", p=128)
                qTv = qT_d.rearrange("(j p) t -> p j t", p=128)
                iqTv = iqT_d.rearrange("(j p) t -> p j t", p=128)
                Vv = V_d.rearrange("h p b d -> p b h d")
                iwv = iw_d.rearrange("b p f -> p b f")
                xv = xview(xT)

                def norms(pt, W, dstmax, j):
                    s_ = sqq[j % 2]
                    kb.A(I("activation", out=s_[:, :W], in_=pt[:, :W], func=AF.Square), [pt], [s_])
                    for hh in range(2):
                        pn = next_ps()
                        kb.PE(I("matmul", pn[:, :W], lhsT=sh.sel[:, hh, :], rhs=s_[:, :W], start=True, stop=True),
                              [sh.sel, s_], [pn])
                        kb.V(I("tensor_reduce", out=mt[:, 2 * j + hh:2 * j + hh + 1], in_=pn[:, :W], op=ALU.max, axis=AX.X),
                             [pn], [mt], partial=True)
                    if j == 3:
                        kb.V(I("tensor_tensor", out=dstmax[:, :], in0=dstmax[:, :], in1=mt[:, :], op=ALU.max), [mt, dstmax], [dstmax])

                for ti in range(2):
                    t0, W, ec = tiles_all[ti]
                    kb.DMA(xf[ti][:, :, :W], xv[:, :, t0:t0 + W], [], [xf[ti]], partial=False)
                load_weight(kb, sh, Wa, l0_in_w[:, 512:768], 8, 256, g, 0)
                load_weight(kb, sh, Wa, l0_in_w[:, 1280:1344], 8, 64, g, 256)
                load_weight(kb, sh, Wa, l0_in_w[:, 1280:1344], 8, 64, g, 320)
                load_weight(kb, sh, Wa, l0_in_w[:, 0:512], 8, 512, g, 384)
                load_weight(kb, sh, Wa, l0_in_w[:, 768:1280], 8, 512, g, 896)
                load_weight(kb, sh, Wa, l0_in_w[:, 1344:1352], 8, 8, g, 1408)
                load_weight(kb, sh, wuk, wukT_d, 2, 512, None)
                load_weight(kb, sh, wuv, wuv_d, 2, 512, None)
                for ti, (t0, W, ec) in enumerate(tiles_all):
                    nblk = W // 128
                    ci0 = t0 // 128
                    x_, n_ = xf[ti % 2], xn[ti % 2]
                    if ti >= 2:
                        kb.DMA(x_[:, :, :W], xv[:, :, t0:t0 + W], [], [x_], partial=False)
                    rmsnorm_fm(kb, sh, x_, W, n_)
                    pk = [next_ps(), next_ps()]
                    for c in range(2):
                        proj(kb, pk[c], Wa, c * 128, n_, W)
                        kb.A(I("activation", out=sqk[:, c, :W], in_=pk[c][:, :W], func=AF.Square), [pk[c]], [sqk], partial=(c > 0))
                    pss = next_ps()
                    for c in range(2):
                        kb.PE(I("matmul", pss[:, :W], lhsT=sh.ones[:, :], rhs=sqk[:, c, :W], start=(c == 0), stop=(c == 1)),
                              [sh.ones, sqk], [pss])
                    rs = sh.rstd[sh.rstd_i % 2]
                    sh.rstd_i += 1
                    kb.A(I("activation", out=rs[:, :W], in_=pss[:, :W], func=AF.Sqrt, scale=1.0 / 256.0, bias=sh.eps[:, 0:1]),
                         [pss, sh.eps], [rs])
                    kb.V(I("reciprocal", out=rs[:, :W], in_=rs[:, :W]), [rs], [rs])
                    for c in range(2):
                        kb.V(I("scalar_tensor_tensor", out=ckvn[:, c, :W], in0=pk[c][:, :W], scalar=sh.v2[:, c:c + 1], in1=rs[:, :W],
                                                                    op0=ALU.mult, op1=ALU.mult), [pk[c], sh.v2, rs], [ckvn], partial=(c > 0))
                    if LVL < 2:
                        continue
                    k_ = kT_sb[ti % 2]
                    for j in range(4):
                        pt = next_ps()
                        for c in range(2):
                            kb.PE(I("matmul", pt[:, :W], lhsT=wuk[:, c, j * 128:(j + 1) * 128], rhs=ckvn[:, c, :W],
                                                                      start=(c == 0), stop=(c == 1)), wuk.rb(j * 128, (j + 1) * 128) + [ckvn], [pt])
                        kb.A(I("activation", out=k_[:, j, :W], in_=pt[:, :W], func=AF.Copy), [pt], [k_], partial=(j > 0))
                        norms(pt, W, sh.kmax, j)
                    kb.DMA(kTv[:, :, t0:t0 + W], k_[:, :, :W], [k_], [], eng="gpsimd")
                    if LVL < 3:
                        continue
                    v_ = V_sb[ti % 2]
                    for blk in range(nblk):
                        pv = next_ps()
                        for c in range(2):
                            kb.PE(I("matmul", pv[:, :512], lhsT=ckvn[:, c, blk * 128:(blk + 1) * 128], rhs=wuv[:, c, :],
                                                                          start=(c == 0), stop=(c == 1)), wuv.rb(0, 512) + [ckvn], [pv])
                        kb.A(I("activation", out=v_[:, blk, :, 0:64], in_=pv[:, :512].rearrange("p (h d) -> p h d", h=8),
                                                                    func=AF.Copy), [pv], [v_], partial=(blk > 0))
                    for h in range(8):
                        kb.DMA(V_d[h, :, ci0:ci0 + nblk, :], v_[:, :nblk, h, :], [v_], [], eng="gpsimd")
                    if LVL < 4:
                        continue
                    pik = next_ps()
                    proj(kb, pik, Wa, 256, n_, W)
                    i_ = ik_sb[ti % 2]
                    kb.A(I("activation", out=i_[:, :W], in_=pik[:, :W], func=AF.Copy), [pik], [i_])
                    kb.DMA(ikT_d[:, t0:t0 + W], i_[:, :W], [i_], [], eng="gpsimd")
                    if ec is None or LVL < 5:
                        continue
                    q_, iq_, w_ = qT_sb[ti % 2], iq_sb[ti % 2], iw_sb[ti % 2]
                    for j in range(4):
                        pq = next_ps()
                        proj(kb, pq, Wa, 384 + j * 128, n_, W)
                        kb.A(I("activation", out=q_[:, j, :W], in_=pq[:, :W], func=AF.Copy), [pq], [q_], partial=(j > 0))
                        norms(pq, W, sh.qmax, j)
                    for j in range(4):
                        pq = next_ps()
                        proj(kb, pq, Wa, 896 + j * 128, n_, W)
                        kb.A(I("activation", out=iq_[:, j, :W], in_=pq[:, :W], func=AF.Copy), [pq], [iq_], partial=(j > 0))
                    if LVL < 6:
                        continue
                    pw = next_ps()
                    for blk in range(nblk):
                        for k in range(8):
                            kb.PE(I("matmul", pw[:, blk * 8:blk * 8 + 8], lhsT=n_[:, k, blk * 128:(blk + 1) * 128],
                                                                   rhs=Wa[:, k, 1408:1416], start=(k == 0), stop=(k == 7)), Wa.rb(1408, 1416) + [n_], [pw])
                    pwv = pw[:, 0:nblk * 8].rearrange("p (b h) -> p b h", h=8)
                    kb.A(I("activation", out=w_[:, :nblk, 0:8], in_=pwv, func=AF.Abs), [pw], [w_])
                    kb.A(I("activation", out=w_[:, :nblk, 8:16], in_=pwv, func=AF.Sign), [pw], [w_], partial=True)
                    kb.DMA(qTv[:, :, ec:ec + W], q_[:, :, :W], [q_], [], eng="gpsimd")
                    kb.DMA(iqTv[:, :, ec:ec + W], iq_[:, :, :W], [iq_], [], eng="gpsimd")
                    kb.DMA(iwv[:, ec // 128:ec // 128 + nblk, :], w_[:, :nblk, :], [w_], [], eng="gpsimd")
                if DBG:
                    mxs = kb.sb(pes, "mxs", [128, 16], F32)
                    kb.V(I("tensor_copy", out=mxs[:, 0:8], in_=sh.kmax[:, :]), [sh.kmax], [mxs])
                    kb.V(I("tensor_copy", out=mxs[:, 8:16], in_=sh.qmax[:, :]), [sh.qmax], [mxs], partial=True)
                    kb.DMA(mx_d[:, :], mxs[:, :], [mxs], [])
                P.barrier()

        def phase_1b():
            with ExitStack() as pes:
                common_tiles(pes)
                sh.nps = 4
                v2 = sh.v2
                Wb = kb.sb(pes, "Wb", [128, 8, 2576], BF16)
                g = gain("l0_attn_norm")
                dgc = kb.sb(pes, "dgc", [128, 12, 4, 128], BF16)
                for n in range(12):
                    for k in range(4):
                        kb.V(I("tensor_scalar", out=dgc[:, n, k, :], in0=cst[:, 0:128], scalar1=v2[:, 2 + n * 4 + k:3 + n * 4 + k],
                               scalar2=None, op0=ALU.mult), [cst, v2], [dgc], partial=True)
                Aneg = kb.sb(pes, "Aneg", [128, 16], F32)
                kb.A(I("activation", out=Aneg[:, :], in_=v2[:, 86:102], func=AF.Exp), [v2], [Aneg])
                kb.V(I("tensor_scalar", out=Aneg[:, :], in0=Aneg[:, :], scalar1=-1.0, scalar2=None, op0=ALU.mult), [Aneg], [Aneg])
                xf = kb.sb(pes, "xf", [128, 8, TW], F32)
                xn = [kb.sb(pes, "xn%d" % i, [128, 8, TW], BF16) for i in range(2)]
                xr = kb.sb(pes, "xr", [128, 12, TW + 3], BF16)
                cr = kb.sb(pes, "cr", [128, 12, 3], BF16)
                xc = kb.sb(pes, "xc", [128, 12, TW], BF16)
                dts = kb.sb(pes, "dts", [128, 3, 16], F32)
                ets = kb.sb(pes, "ets", [128, 3, 16], F32)
                a_sb = kb.sb(pes, "a_sb", [128, 3, 16], F32)
                Sst = kb.sb(pes, "Sst", [128, 1024], F32)
                S_bf = kb.sb(pes, "S_bf", [128, 1024], BF16)
                tmpS = kb.sb(pes, "tmpS", [128, 1024], F32)
                acs = kb.sb(pes, "acs", [128, 16], F32)
                Ee = kb.sb(pes, "Ee", [128, 16], F32)
                dd = kb.sb(pes, "dd", [128, 16], F32)
                dend = kb.sb(pes, "dend", [128, 16], F32)
                cdec = kb.sb(pes, "cdec", [128, 16], F32)
                w1 = kb.sb(pes, "w1", [128, 16], F32)
                xs_sb = kb.sb(pes, "xs_sb", [128, 1024], BF16)
                xdd = kb.sb(pes, "xdd", [128, 1024], BF16)
                xdt = kb.sb(pes, "xdt", [128, 1024], BF16)
                B_sb = kb.sb(pes, "B_sb", [128, 2, 128], BF16)
                cb_sb = kb.sb(pes, "cb_sb", [128, 2, 128], BF16)
                zs = kb.sb(pes, "zs", [128, 1024], BF16)
                seg = [kb.sb(pes, "seg%d" % i, [128, 4, 128], F32) for i in range(2)]
                dec = kb.sb(pes, "dec", [128, 16, 128], BF16)
                MT = kb.sb(pes, "MT", [128, 16, 128], BF16)
                t1 = kb.sb(pes, "t1", [128, 1024], F32)
                t2 = kb.sb(pes, "t2", [128, 1024], F32)
                t3 = kb.sb(pes, "t3", [128, 1024], F32)
                yn = kb.sb(pes, "yn", [128, 1024], BF16)
                junk = kb.sb(pes, "junk", [128, 512], BF16)
                ssq = kb.sb(pes, "ssq", [128, 2], F32)
                rsd = kb.sb(pes, "rsd", [128, 2], F32)
                ybT_sb = [kb.sb(pes, "ybT%d" % i, [128, 8, 128], BF16) for i in range(2)]
                kb.V(I("memset", cr[:, :, :], 0.0), [], [cr])
                kb.V(I("memset", Sst[:, :], 0.0), [], [Sst])
                xv = xview(xT)
                ybv = ybT_d.rearrange("(k p) t -> p k t", p=128)
                tri = cst[:, 128:256]
                negtri = cst[:, 256:384]
                onesf = cst[:, 384:512]
                identf = cst[:, 0:128]

                def b3(ap, n, m):
                    return ap.unsqueeze(2).to_broadcast([128, n, m])

                kb.DMA(xf[:, :, :tiles_all[0][1]], xv[:, :, 0:tiles_all[0][1]], [], [xf], partial=False)
                load_weight(kb, sh, Wb, l0_in_w[:, 2376:3928], 8, 1552, g, 0)
                load_weight(kb, sh, Wb, l0_in_w[:, 1352:2376], 8, 1024, g, 1552)
                for ti, (t0, W, ec) in enumerate(tiles_all):
                    nblk = W // 128
                    n_ = xn[ti % 2]
                    if ti >= 1:
                        kb.DMA(xf[:, :, :W], xv[:, :, t0:t0 + W], [], [xf], partial=False)
                    rmsnorm_fm(kb, sh, xf, W, n_)
                    kb.V(I("tensor_copy", out=xr[:, :, 0:3], in_=cr[:, :, :]), [cr], [xr])
                    for n in range(12):
                        px = next_ps()
                        proj(kb, px, Wb, n * 128, n_, W)
                        kb.A(I("activation", out=xr[:, n, 3:W + 3], in_=px[:, :W], func=AF.Copy), [px], [xr], partial=True)
                    kb.V(I("tensor_copy", out=cr[:, :, :], in_=xr[:, :, W:W + 3]), [xr], [cr])
                    for n in range(12):
                        pc = next_ps()
                        for k in range(4):
                            kb.PE(I("matmul", pc[:, :W], lhsT=dgc[:, n, k, :], rhs=xr[:, n, k:k + W], start=(k == 0), stop=(k == 3)),
                                  [dgc, xr], [pc])
                        kb.A(I("activation", out=xc[:, n, :W], in_=pc[:, :W], func=AF.Silu, bias=v2[:, 50 + n:51 + n]), [pc, v2], [xc],
                             partial=(n > 0))
                    pdt = next_ps()
                    for blk in range(nblk):
                        for k in range(8):
                            kb.PE(I("matmul", pdt[:, blk * 16:blk * 16 + 16], lhsT=n_[:, k, blk * 128:(blk + 1) * 128],
                                    rhs=Wb[:, k, 1536:1552], start=(k == 0), stop=(k == 7)), Wb.rb(1536, 1552) + [n_], [pdt])
                    kb.V(I("tensor_tensor", out=dts[:, :nblk, :], in0=pdt[:, 0:nblk * 16].rearrange("p (b h) -> p b h", h=16),
                           in1=v2[:, 70:86].unsqueeze(1).to_broadcast([128, nblk, 16]), op=ALU.add), [pdt, v2], [dts])
                    kb.A(I("activation", out=ets[:, :nblk, :], in_=dts[:, :nblk, :], func=AF.Exp), [dts], [ets])
                    kb.A(I("activation", out=dts[:, :nblk, :], in_=ets[:, :nblk, :], func=AF.Ln, bias=1.0), [ets], [dts])
                    kb.V(I("tensor_tensor", out=a_sb[:, :nblk, :], in0=dts[:, :nblk, :],
                           in1=Aneg[:, :].unsqueeze(1).to_broadcast([128, nblk, 16]), op=ALU.mult), [dts, Aneg], [a_sb])
                    for blk in range(nblk):
                        ci = t0 // 128 + blk
                        c0 = blk * 128
                        full = ci >= 31
                        if ci == 32:
                            kb.V(I("tensor_scalar", out=Sst[:, :], in0=Sst[:, :], scalar1=sh.flags[:, 0:1], scalar2=None, op0=ALU.mult),
                                 [Sst, sh.flags], [Sst])
                        pT = next_ps()
                        pTb = pT.bf()
                        for n in range(8):
                            kb.PE(I("transpose", out=pTb[:, n * 128:(n + 1) * 128], in_=xc[:, n, c0:c0 + 128], identity=sh.ident[:, :]),
                                  [xc, sh.ident], [pT])
                        pT2 = next_ps()
                        pT2b = pT2.bf()
                        for gg in range(2):
                            kb.PE(I("transpose", out=pT2b[:, gg * 128:(gg + 1) * 128], in_=xc[:, 8 + gg, c0:c0 + 128], identity=sh.ident[:, :]),
                                  [xc, sh.ident], [pT2])
                        pcs = next_ps()
                        kb.PE(I("matmul", pcs[:, 0:16], lhsT=tri, rhs=a_sb[:, blk, :], start=True, stop=True), [cst, a_sb], [pcs])
                        kb.PE(I("matmul", pcs[:, 16:32], lhsT=onesf, rhs=a_sb[:, blk, :], start=True, stop=True), [cst, a_sb], [pcs])
                        kb.A(I("activation", out=acs[:, :], in_=pcs[:, 0:16], func=AF.Copy), [pcs], [acs])
                        if full:
                            kb.A(I("activation", out=Ee[:, :], in_=pcs[:, 0:16], func=AF.Exp), [pcs], [Ee])
                        kb.V(I("tensor_tensor", out=dd[:, :], in0=pcs[:, 16:32], in1=acs[:, :], op=ALU.subtract), [pcs, acs], [dd])
                        kb.A(I("activation", out=dend[:, :], in_=dd[:, :], func=AF.Exp), [dd], [dend])
                        kb.A(I("activation", out=cdec[:, :], in_=pcs[:, 16:32], func=AF.Exp), [pcs], [cdec])
                        kb.V(I("tensor_tensor", out=w1[:, :], in0=dts[:, blk, :], in1=dend[:, :], op=ALU.mult), [dts, dend], [w1])
                        kb.A(I("activation", out=xs_sb[:, :], in_=pTb[:, :], func=AF.Copy), [pT], [xs_sb])
                        kb.G(I("tensor_tensor", out=xdd[:, :].rearrange("p (h d) -> p h d", h=16),
                               in0=xs_sb[:, :].rearrange("p (h d) -> p h d", h=16), in1=b3(w1[:, :], 16, 64), op=ALU.mult),
                             [xs_sb, w1], [xdd])
                        kb.A(I("activation", out=B_sb[:, :, :], in_=pT2b[:, 0:256].rearrange("p (g n) -> p g n", g=2), func=AF.Copy),
                             [pT2], [B_sb])
                        pst = next_pd()
                        for gg in range(2):
                            kb.PE(I("matmul", pst[:, gg * 512:(gg + 1) * 512], lhsT=B_sb[:, gg, :], rhs=xdd[:, gg * 512:(gg + 1) * 512],
                                    start=True, stop=True), [B_sb, xdd], [pst])
                        if full:
                            kb.G(I("tensor_copy", out=S_bf[:, :], in_=Sst[:, :]), [Sst], [S_bf])
                        kb.G(I("tensor_tensor", out=tmpS[:, :].rearrange("p (h d) -> p h d", h=16),
                               in0=Sst[:, :].rearrange("p (h d) -> p h d", h=16), in1=b3(cdec[:, :], 16, 64), op=ALU.mult),
                             [Sst, cdec], [tmpS])
                        kb.V(I("tensor_tensor", out=Sst[:, :], in0=pst[:, :], in1=tmpS[:, :], op=ALU.add), [pst, tmpS], [Sst])
                        if not full:
                            continue
                        ecol = (ci - 31) * 128
                        pz = next_pd()
                        for jz in range(2):
                            for k in range(8):
                                kb.PE(I("matmul", pz[:, jz * 512:(jz + 1) * 512], lhsT=n_[:, k, c0:c0 + 128],
                                        rhs=Wb[:, k, 1552 + jz * 512:1552 + (jz + 1) * 512], start=(k == 0), stop=(k == 7)), Wb.rb(1552 + jz * 512, 1552 + (jz + 1) * 512) + [n_], [pz])
                        kb.A(I("activation", out=zs[:, :], in_=pz[:, :], func=AF.Silu), [pz], [zs])
                        pyo = next_pd()
                        for gg in range(2):
                            kb.PE(I("matmul", pyo[:, gg * 512:(gg + 1) * 512], lhsT=xc[:, 10 + gg, c0:c0 + 128],
                                    rhs=S_bf[:, gg * 512:(gg + 1) * 512], start=True, stop=True), [xc, S_bf], [pyo])
                        kb.V(I("tensor_tensor", out=t1[:, :].rearrange("p (h d) -> p h d", h=16),
                               in0=pyo[:, :].rearrange("p (h d) -> p h d", h=16), in1=b3(Ee[:, :], 16, 64), op=ALU.mult), [pyo, Ee], [t1])
                        pcb = next_ps()
                        for gg in range(2):
                            kb.PE(I("matmul", pcb[:, gg * 128:(gg + 1) * 128], lhsT=xc[:, 8 + gg, c0:c0 + 128], rhs=xc[:, 10 + gg, c0:c0 + 128],
                                    start=True, stop=True), [xc], [pcb])
                        kb.A(I("activation", out=cb_sb[:, :, :], in_=pcb[:, 0:256].rearrange("p (g n) -> p g n", g=2), func=AF.Copy),
                             [pcb], [cb_sb])
                        kb.G(I("tensor_tensor", out=xdt[:, :].rearrange("p (h d) -> p h d", h=16),
                               in0=xs_sb[:, :].rearrange("p (h d) -> p h d", h=16), in1=b3(dts[:, blk, :], 16, 64), op=ALU.mult),
                             [xs_sb, dts], [xdt])
                        for hq in range(4):
                            pR = next_ps()
                            for hh in range(4):
                                h = hq * 4 + hh
                                kb.PE(I("matmul", pR[:, hh * 128:(hh + 1) * 128], lhsT=a_sb[:, blk, h:h + 1].to_broadcast([128, 128]), rhs=tri,
                                        start=True, stop=False), [a_sb, cst], [pR])
                                kb.PE(I("matmul", pR[:, hh * 128:(hh + 1) * 128], lhsT=identf, rhs=negtri, start=False, stop=True), [cst], [pR])
                            sg_ = seg[hq % 2]
                            kb.V(I("tensor_tensor", out=sg_[:, :, :], in0=pR[:, :].rearrange("p (h l) -> p h l", h=4),
                                   in1=b3(acs[:, hq * 4:hq * 4 + 4], 4, 128), op=ALU.subtract), [pR, acs], [sg_])
                            kb.A(I("activation", out=dec[:, hq * 4:hq * 4 + 4, :], in_=sg_[:, :, :], func=AF.Exp), [sg_], [dec], partial=(hq > 0))
                        for gg in range(2):
                            kb.G(I("tensor_tensor", out=MT[:, gg * 8:(gg + 1) * 8, :], in0=dec[:, gg * 8:(gg + 1) * 8, :],
                                   in1=cb_sb[:, gg, :].unsqueeze(1).to_broadcast([128, 8, 128]), op=ALU.mult), [dec, cb_sb], [MT], partial=(gg > 0))
                        pyd = next_pd()
                        for h in range(16):
                            kb.PE(I("matmul", pyd[:, h * 64:(h + 1) * 64], lhsT=MT[:, h, :], rhs=xdt[:, h * 64:(h + 1) * 64], start=True, stop=True),
                                  [MT, xdt], [pyd])
                        kb.V(I("tensor_tensor", out=t2[:, :], in0=pyd[:, :], in1=t1[:, :], op=ALU.add), [pyd, t1], [t2])
                        kb.G(I("tensor_tensor", out=t3[:, :].rearrange("p (h d) -> p h d", h=16),
                               in0=xs_sb[:, :].rearrange("p (h d) -> p h d", h=16), in1=b3(v2[:, 102:118], 16, 64), op=ALU.mult), [xs_sb, v2], [t3])
                        kb.G(I("tensor_tensor", out=t3[:, :], in0=t3[:, :], in1=t2[:, :], op=ALU.add), [t3, t2], [t3])
                        kb.G(I("tensor_tensor", out=t3[:, :], in0=t3[:, :], in1=zs[:, :], op=ALU.mult), [t3, zs], [t3])
                        for gg in range(2):
                            kb.A(I("activation", out=junk[:, :], in_=t3[:, gg * 512:(gg + 1) * 512], func=AF.Square, accum_out=ssq[:, gg:gg + 1]),
                                 [t3], [junk, ssq], partial=(gg > 0))
                        kb.A(I("activation", out=rsd[:, :], in_=ssq[:, :], func=AF.Sqrt, scale=1.0 / 512.0, bias=sh.eps[:, 0:1]), [ssq, sh.eps], [rsd])
                        kb.V(I("reciprocal", out=rsd[:, :], in_=rsd[:, :]), [rsd], [rsd])
                        for gg in range(2):
                            kb.V(I("tensor_scalar", out=yn[:, gg * 512:(gg + 1) * 512], in0=t3[:, gg * 512:(gg + 1) * 512], scalar1=rsd[:, gg:gg + 1],
                                   scalar2=None, op0=ALU.mult), [t3, rsd], [yn], partial=(gg > 0))
                        pY = next_ps()
                        pYb = pY.bf()
                        for n in range(8):
                            kb.PE(I("transpose", out=pYb[:, n * 128:(n + 1) * 128], in_=yn[:, n * 128:(n + 1) * 128], identity=sh.ident[:, :]),
                                  [yn, sh.ident], [pY])
                        yb_ = ybT_sb[ci % 2]
                        kb.V(I("tensor_tensor", out=yb_[:, :, :], in0=pYb[:, :].rearrange("p (n l) -> p n l", n=8),
                               in1=b3(v2[:, 62:70], 8, 128), op=ALU.mult), [pY, v2], [yb_])
                        kb.DMA(ybv[:, :, ecol:ecol + 128], yb_[:, :, :], [yb_], [], eng="gpsimd")
                P.barrier()

        NIT = 16
        FP8 = mybir.dt.float8e4

        def phase_2():
            with ExitStack() as pes:
                sh.nps = 8
                flags_ = sh.flags
                P2NT = int(os.environ.get("P2NT", NT))
                ikT2 = kb.sb(pes, "ikT2", [128, S], BF16)
                kb.DMA(ikT2[:, :], ikT_d[:, :], [], [ikT2], partial=False)
                negM = kb.sb(pes, "negM", [128, 8], F32)
                kb.V(I("tensor_tensor", out=negM[:, :], in0=sh.qmax[:, :], in1=sh.kmax[:, :], op=ALU.add), [sh.qmax, sh.kmax], [negM])
                kb.V(I("tensor_scalar", out=negM[:, :], in0=negM[:, :], scalar1=-1.0 / 16.0, scalar2=None, op0=ALU.mult), [negM], [negM])
                scs = [kb.sb(pes, "sc%d" % i, [128, S], F32) for i in range(2)]
                for t_ in scs:
                    t_.subs = [_B("scsub%d" % j) for j in range(16)]
                mk = kb.sb(pes, "mk", [128, S], BF16)
                maskT = [kb.sb(pes, "maskT%d" % i, [128, 64, TW], FP8) for i in range(2)]
                iq_sb = [kb.sb(pes, "iq_sb%d" % i, [128, 4, 2, TW], BF16) for i in range(2)]
                iw_sb = [kb.sb(pes, "iw_sb%d" % i, [128, 3, 16], F32) for i in range(2)]
                qT_sb = [kb.sb(pes, "qT_sb%d" % i, [128, 8, TW], BF16) for i in range(1)]
                Dg = kb.sb(pes, "Dg", [128, 8, 128], BF16)
                rl = [kb.sb(pes, "rl%d" % i, [128, 512], BF16) for i in range(4)]
                amx = kb.sb(pes, "amx", [128, 16], F32)
                smax = kb.sb(pes, "smax", [128, 1], F32)
                lo = kb.sb(pes, "lo", [128, 1], F32)
                mid = kb.sb(pes, "mid", [128, 1], F32)
                cnt = kb.sb(pes, "cnt", [128, 1], F32)
                ge = kb.sb(pes, "ge", [128, 1], F32)
                wall = kb.sb(pes, "wall", [128, NIT], F32)
                kc = [kb.sb(pes, "kc%d" % i, [128, 2048], BF16) for i in range(2)]
                for t_ in iq_sb:
                    kb.G(I("memset", t_[:, :, :, :], 0.0), [], [t_])
                for t_ in qT_sb + kc:
                    kb.G(I("memset", t_[64:128], 0.0), [], [t_])
                vc = [kb.sb(pes, "vc%d" % i, [128, 16, 65], BF16) for i in range(2)]
                pe_ = [kb.sb(pes, "pe%d" % i, [128, TW], BF16) for i in range(4)]
                o_sb = [kb.sb(pes, "o_sb%d" % i, [65, TW], F32) for i in range(2)]
                rden = [kb.sb(pes, "rden%d" % i, [65, TW], F32) for i in range(2)]
                rb_sb = [kb.sb(pes, "rb%d" % i, [64, TW], F32) for i in range(2)]
                ya_sb = [kb.sb(pes, "ya%d" % i, [64, 8, TW], BF16) for i in range(1)]
                negI = kb.sb(pes, "negI", [128, 128], BF16)
                kb.V(I("tensor_scalar", out=negI[:, :], in0=cst[:, 0:128], scalar1=-4096.0, scalar2=None, op0=ALU.mult), [cst], [negI])
                negtriT = cst[:, 768:896]
                pow2 = cst[:, 896:896 + NIT]
                onesf = cst[:, 384:512]
                iqTv = iqT_d.rearrange("(j two d) t -> two d j t", two=2, d=64)
                qTv = qT_d.rearrange("(h d) t -> d h t", d=64)
                yaTv = yaT_d.rearrange("(h d) t -> d h t", d=64)
                iwv = iw_d.rearrange("b p f -> p b f")
                il_banks = [psums[0], psums[1]]
                p_sc = psums[2]
                p_tr = psums[3]
                lg_banks = [psums[4], psums[5], psums[6]]
                p_acc = psums[7]
                cnt_i = dict(il=0, rl=0, lg=0, pe=0, pm=0, kv=0, hd=0)

                fm_state = {}

                def idx_load(i):
                    e0 = i * TW
                    iq_, iw_ = iq_sb[i % 2], iw_sb[i % 2]
                    kb.DMA(iq_[0:64, :, 0, :], iqTv[0, :, :, e0:e0 + TW], [], [iq_], partial=False)
                    kb.DMA(iq_[64:128, :, 1, :], iqTv[1, :, :, e0:e0 + TW], [], [iq_], partial=True)
                    kb.DMA(iw_[:, :, :], iwv[:, 3 * i:3 * i + 3, :], [], [iw_], partial=False)
                    fm_state[i] = True

                def idx_stream(i, qq0):
                    iq_, iw_, mT = iq_sb[i % 2], iw_sb[i % 2], maskT[i % 2]
                    nsb_t = 34 + 3 * i
                    first_mask = fm_state[i]
                    sc = scs[(3 * i + qq0) % 2]
                    scb = sc.subs
                    for qq in (qq0,):
                        qb = 3 * i + qq
                        nb = 32 + qb
                        L = nb * 128
                        nkt = (L + 511) // 512
                        if nb < nsb_t:
                            kb.G(I("memset", mT[:, nb:nsb_t, qq * 128:(qq + 1) * 128], 1.0), [], [mT], partial=not first_mask)
                            first_mask = False
                            fm_state[i] = False
                        for h in range(8):
                            kb.G(I("tensor_scalar", out=Dg[:, h, :], in0=cst[:, 0:128], scalar1=iw_[:, qq, 8 + h:9 + h], scalar2=1.0,
                                   op0=ALU.mult, op1=ALU.mult), [cst, iw_], [Dg], partial=(h > 0))
                        for kt in range(nkt):
                            w = min(512, L - kt * 512)
                            prev = None
                            for h in range(8):
                                pil = il_banks[cnt_i["il"] % 2]
                                cnt_i["il"] += 1
                                hp = (h % 2) * 64
                                kb.PE(I("matmul", pil[:, :w], lhsT=iq_[:, h // 2, h % 2, qq * 128:(qq + 1) * 128],
                                        rhs=ikT2[:, kt * 512:kt * 512 + w], start=True, stop=True), [iq_, ikT2], [pil])
                                r_ = rl[cnt_i["rl"] % 4]
                                cnt_i["rl"] += 1
                                kb.A(I("activation", out=r_[:, :w], in_=pil[:, :w], func=AF.Relu, scale=iw_[:, qq, h:h + 1]), [pil, iw_], [r_])
                                if prev is not None:
                                    ph, pr = prev
                                    kb.PE(I("matmul", p_sc[:, :w], lhsT=Dg[:, ph, :], rhs=pr[:, :w], start=(ph == 0), stop=False), [Dg, pr], [p_sc])
                                prev = (h, r_)
                            ph, pr = prev
                            kb.PE(I("matmul", p_sc[:, :w], lhsT=Dg[:, ph, :], rhs=pr[:, :w], start=False, stop=True), [Dg, pr], [p_sc])
                            c0 = kt * 512
                            kb.A(I("activation", out=sc[:, c0:c0 + w], in_=p_sc[:, :w], func=AF.Copy), [p_sc], [scb[kt]])
                            kb.V(I("tensor_reduce", out=amx[:, kt:kt + 1], in_=sc[:, c0:c0 + w], op=ALU.max, axis=AX.X, apply_absolute_value=True),
                                 [scb[kt]], [amx], partial=(kt > 0))
                            if kt < 8:
                                kb.V(I("tensor_scalar", out=sc[:, c0:c0 + w], in0=sc[:, c0:c0 + w], scalar1=flags_[:, 1:2], scalar2=None, op0=ALU.add),
                                     [scb[kt], flags_], [scb[kt]], partial=True)
                            yield 3.0
                        kb.V(I("tensor_tensor", out=sc[:, L - 128:L], in0=sc[:, L - 128:L], in1=negtriT, op=ALU.add), [scb[nkt - 1], cst], [scb[nkt - 1]], partial=True)
                        yield ("idx_done", nkt * 8 * 0.6)
                        kb.V(I("tensor_reduce", out=smax[:, :], in_=amx[:, :nkt], op=ALU.max, axis=AX.X), [amx], [smax])
                        kb.V(I("tensor_scalar", out=smax[:, :], in0=smax[:, :], scalar1=1.0, scalar2=None, op0=ALU.add), [smax], [smax])
                        kb.V(I("tensor_scalar", out=lo[:, :], in0=smax[:, :], scalar1=-1.0, scalar2=None, op0=ALU.mult), [smax], [lo])
                        kb.V(I("tensor_scalar", out=wall[:, :], in0=pow2, scalar1=smax[:, 0:1], scalar2=2.0, op0=ALU.mult, op1=ALU.mult),
                             [cst, smax], [wall])
                        for it in range(NIT):
                            kb.V(I("tensor_tensor", out=mid[:, :], in0=lo[:, :], in1=wall[:, it:it + 1], op=ALU.add), [lo, wall], [mid])
                            kb.V(I("tensor_scalar", out=mk[:, :L], in0=sc[:, :L], scalar1=mid[:, 0:1], scalar2=None, op0=ALU.is_ge, op1=ALU.add,
                                   accum_out=cnt[:, 0:1]), scb[:nkt] + [mid], [mk, cnt])
                            kb.V(I("tensor_scalar", out=ge[:, :], in0=cnt[:, :], scalar1=255.5, scalar2=None, op0=ALU.is_ge), [cnt], [ge])
                            kb.V(I("scalar_tensor_tensor", out=lo[:, :], in0=ge[:, :], scalar=wall[:, it:it + 1], in1=lo[:, :], op0=ALU.mult, op1=ALU.add),
                                 [ge, wall, lo], [lo])
                        kb.V(I("tensor_scalar", out=mk[:, :L], in0=sc[:, :L], scalar1=lo[:, 0:1], scalar2=None, op0=ALU.is_lt), scb[:nkt] + [lo], [mk])
                        yield ("bis_done", (NIT + 1) * (L / 960.0 + 1.0))
                        for s0 in range(0, nb, 8):
                            ns = min(8, nb - s0)
                            ptb = p_tr.bf()
                            for j in range(ns):
                                kb.PE(I("transpose", out=ptb[:, j * 128:(j + 1) * 128], in_=mk[:, (s0 + j) * 128:(s0 + j + 1) * 128], identity=sh.ident[:, :]),
                                      [mk, sh.ident], [p_tr])
                            kb.A(I("activation", out=mT[:, s0:s0 + ns, qq * 128:(qq + 1) * 128],
                                   in_=ptb[:, 0:ns * 128].rearrange("p (b t) -> p b t", t=128), func=AF.Copy, saturate=False), [p_tr], [mT], partial=not first_mask)
                            first_mask = False
                            fm_state[i] = False
                            yield 1.2

                def attn_stream(i):
                    e0 = i * TW
                    nsb_t = 34 + 3 * i
                    q_, mT, ya_ = qT_sb[0], maskT[i % 2], ya_sb[0]
                    kb.DMA(q_[0:64, :, :], qTv[:, :, e0:e0 + TW], [], [q_], partial=False)
                    for h in range(8):
                        pend = []
                        its = []
                        for cb0 in range(0, nsb_t, 16):
                            ncb = min(16, nsb_t - cb0)
                            kc_, vc_ = kc[cnt_i["kv"] % 2], vc[cnt_i["kv"] % 2]
                            cnt_i["kv"] += 1
                            kb.DMA(kc_[0:64, :ncb * 128], kT_d[h * 64:(h + 1) * 64, cb0 * 128:(cb0 + ncb) * 128], [], [kc_], partial=False)
                            kb.DMA(vc_[:, :ncb, :], V_d[h, :, cb0:cb0 + ncb, :], [], [vc_], partial=False)
                            for j in range(ncb):
                                sb = cb0 + j
                                plg = lg_banks[cnt_i["lg"] % 3]
                                cnt_i["lg"] += 1
                                kb.PE(I("matmul", plg[:, :TW], lhsT=kc_[:, j * 128:(j + 1) * 128], rhs=q_[:, h, :], start=True, stop=False),
                                      [kc_, q_], [plg])
                                kb.PE(I("matmul", plg[:, :TW], lhsT=negI[:, :], rhs=mT[:, sb, :], start=False, stop=True), [negI, mT], [plg])
                                e_ = pe_[cnt_i["pe"] % 4]
                                cnt_i["pe"] += 1
                                kb.A(I("activation", out=e_[:, :], in_=plg[:, :TW], func=AF.Exp, scale=0.125, bias=negM[:, h:h + 1]), [plg, negM], [e_])
                                pend.append((sb, vc_, j, e_))
                                if len(pend) > 2:
                                    sb2, v2_, j2, m2 = pend.pop(0)
                                    kb.PE(I("matmul", p_acc[0:65, :TW], lhsT=v2_[:, j2, :], rhs=m2[:, :], start=(sb2 == 0), stop=(sb2 == nsb_t - 1)),
                                          [v2_, m2], [p_acc])
                                if sb % 4 == 3:
                                    yield 2.4
                        for sb2, v2_, j2, m2 in pend:
                            kb.PE(I("matmul", p_acc[0:65, :TW], lhsT=v2_[:, j2, :], rhs=m2[:, :], start=(sb2 == 0), stop=(sb2 == nsb_t - 1)),
                                  [v2_, m2], [p_acc])
                        hd = cnt_i["hd"] % 2
                        cnt_i["hd"] += 1
                        o_, rd_, rb_ = o_sb[hd], rden[hd], rb_sb[hd]
                        kb.A(I("activation", out=o_[:, :], in_=p_acc[0:65, :TW], func=AF.Copy), [p_acc], [o_])
                        kb.G(I("tensor_scalar", out=rd_[64:65, :], in0=o_[64:65, :], scalar1=1e-30, scalar2=1.0, op0=ALU.max, op1=ALU.mult), [o_], [rd_])
                        kb.A(I("activation", out=rd_[64:65, :], in_=rd_[64:65, :], func=AF.Ln), [rd_], [rd_])
                        kb.A(I("activation", out=rd_[64:65, :], in_=rd_[64:65, :], func=AF.Exp, scale=-1.0), [rd_], [rd_])
                        kb.PE(I("matmul", p_tr[0:64, :TW], lhsT=onesf[64:65, 0:64], rhs=rd_[64:65, :], start=True, stop=True), [cst, rd_], [p_tr])
                        kb.A(I("activation", out=rb_[:, :], in_=p_tr[0:64, :TW], func=AF.Copy), [p_tr], [rb_])
                        kb.G(I("tensor_tensor", out=ya_[:, h, :], in0=o_[0:64, :], in1=rb_[:, :], op=ALU.mult), [o_, rb_], [ya_], partial=(h > 0))
                        yield 3.0
                    kb.DMA(yaTv[:, :, e0:e0 + TW], ya_[:, :, :], [ya_], [], eng="gpsimd")

                def drain(g):
                    for _ in g:
                        pass

                def run_until(g, tag):
                    for r in g:
                        if isinstance(r, tuple) and r[0] == tag:
                            return r[1]
                    return 0.0

                def fill(ga, t):
                    if ga is None:
                        return
                    while t > 0:
                        try:
                            t -= next(ga)
                        except StopIteration:
                            return

                BISF = float(os.environ.get("BISF", "0.9"))

                def tile_sched(i, ga):
                    idx_load(i)
                    g = [idx_stream(i, qq) for qq in range(3)]
                    run_until(g[0], "idx_done")
                    tb0 = run_until(g[0], "bis_done")
                    ti1 = run_until(g[1], "idx_done")
                    fill(ga, (tb0 - ti1) * BISF)
                    drain(g[0])
                    tb1 = run_until(g[1], "bis_done")
                    ti2 = run_until(g[2], "idx_done")
                    fill(ga, (tb1 - ti2) * BISF)
                    drain(g[1])
                    tb2 = run_until(g[2], "bis_done")
                    fill(ga, tb2 * BISF)
                    drain(g[2])
                    if ga is not None:
                        drain(ga)

                tile_sched(0, None)
                for i in range(P2NT):
                    if i + 1 < P2NT:
                        tile_sched(i + 1, attn_stream(i))
                    else:
                        drain(attn_stream(i))
                P.barrier()

        def phase_3():
            with ExitStack() as pes:
                common_tiles(pes)
                sh.nps = 8
                wo = kb.sb(pes, "wo0", [128, 12, D], BF16)
                xf = [kb.sb(pes, "xf%d" % i, [128, 8, TW], F32) for i in range(2)]
                xo = [kb.sb(pes, "xo%d" % i, [128, 8, TW], F32) for i in range(2)]
                ya = [kb.sb(pes, "yat%d" % i, [128, 4, TW], BF16) for i in range(2)]
                yb = [kb.sb(pes, "ybt%d" % i, [128, 8, TW], BF16) for i in range(2)]
                sv = xview(xT[:, EXT0:S])
                dv = xview(xa_d)
                yav = yaT_d.rearrange("(k p) t -> p k t", p=128)
                ybv = ybT_d.rearrange("(k p) t -> p k t", p=128)
                def p3_pre(i):
                    t0 = i * TW
                    x_, a_, b_ = xf[i % 2], ya[i % 2], yb[i % 2]
                    kb.DMA(x_[:, :, :], sv[:, :, t0:t0 + TW], [], [x_], partial=False)
                    kb.DMA(a_[:, :, :], yav[:, :, t0:t0 + TW], [], [a_], partial=False)
                    kb.DMA(b_[:, :, :], ybv[:, :, t0:t0 + TW], [], [b_], partial=False)

                p3_pre(0)
                p3_pre(1)
                load_weight(kb, sh, wo, l0_out_w, 12, D, None, act_share=True)
                for i in range(NT):
                    t0 = i * TW
                    x_, o_, a_, b_ = xf[i % 2], xo[i % 2], ya[i % 2], yb[i % 2]
                    if i >= 2:
                        p3_pre(i)
                    for n in range(8):
                        ps = next_ps()
                        proj(kb, ps, wo, n * 128, a_, TW, nk=4, k0=0, start=True, stop=False)
                        proj(kb, ps, wo, n * 128, b_, TW, nk=8, k0=4, start=False, stop=True)
                        kb.V(I("tensor_tensor", out=o_[:, n, :], in0=ps[:, :TW], in1=x_[:, n, :], op=ALU.add), [ps, x_], [o_], partial=(n > 0))
                    kb.DMA(dv[:, :, t0:t0 + TW], o_[:, :, :], [o_], [], eng="gpsimd")
                P.barrier()

        sh.outbuf = T(None, "outbuf")
        P.barrier()
        if debug == "p1a":
            phase_1a()
        if debug is None:
            phase_1a()
            phase_1b()
            phase_2()
            phase_3()
            phase_ffn_a(xa_d, ffn_w[0][0], ffn_w[0][1], "l0_ffn_norm")
            phase_ffn_b(xa_d, ffn_w[0][2], xb_d)
            phase_l1_mixer(xb_d, xc_d)
            phase_ffn_a(xc_d, ffn_w[1][0], ffn_w[1][1], "l1_ffn_norm")
            phase_ffn_b(xc_d, ffn_w[1][2], None, final_gain="final_norm")
        if debug == "p2":
            phase_1a()
            phase_2()
        if debug == "p1b":
            phase_1b()
        if debug == "l1only":
            xsrc = xT[:, EXT0:S]
            phase_l1_mixer(xsrc, xc_d)
            phase_ffn_a(xc_d, ffn_w[1][0], ffn_w[1][1], "l1_ffn_norm")
            phase_ffn_b(xc_d, ffn_w[1][2], None, final_gain="final_norm")
        P.barrier()
        print("ops", P.nops, "waits", P.nwaits, {k: v for k, v in P.cnt.items()})
        P.emit()
    return nc


def host_consts():
    c = np.zeros((128, 1024), np.float32)
    c[:, 0:128] = np.eye(128, dtype=np.float32)
    s = np.arange(128)
    c[:, 128:256] = (s[:, None] <= s[None, :]).astype(np.float32)
    c[:, 256:384] = np.where(s[:, None] <= s[None, :], 0.0, -30000.0)
    c[:, 384:512] = 1.0
    c[:, 768:896] = np.where(s[None, :] <= s[:, None], 0.0, NEG)
    c[:, 896:896 + 32] = (2.0 ** -(np.arange(32) + 1.0))[None, :]
    c[:64, 512:640] = 1.0
    c[64:, 640:768] = 1.0
    return c


def pp(v):
    return np.ascontiguousarray(v.reshape(-1, 128).T)


def host_vecs(inp):
    v = np.zeros((128, 64), np.float32)
    v[:, 0:8] = pp(inp["l0_attn_norm"])
    v[:, 8:16] = pp(inp["l0_ffn_norm"])
    v[:, 16:24] = pp(inp["l1_conv_norm"])
    v[:, 24:32] = pp(inp["l1_ffn_norm"])
    v[:, 32:40] = pp(inp["final_norm"])
    cw = inp["l1_conv_w"]
    v[:, 40:64] = cw.reshape(3, 8, 128).transpose(2, 1, 0).reshape(128, 24)
    return v


def host_vecs2(inp):
    v = np.zeros((128, 128), np.float32)
    v[:, 0:2] = pp(inp["l0_kv_norm"])
    v[:, 2:50] = inp["l0_conv_w"].reshape(4, 12, 128).transpose(2, 1, 0).reshape(128, 48)
    v[:, 50:62] = pp(inp["l0_conv_b"])
    v[:, 62:70] = pp(inp["l0_ssm_norm"])
    v[:, 70:86] = inp["l0_dt_bias"][None, :]
    v[:, 86:102] = inp["l0_A_log"][None, :]
    v[:, 102:118] = inp["l0_D"][None, :]
    return v


def make_in_maps(inp):
    x = inp["x"]
    maps = []
    consts = host_consts()
    vecs = host_vecs(inp)
    vecs2 = host_vecs2(inp)
    wukT = np.ascontiguousarray(inp["l0_w_uk"].transpose(2, 0, 1).reshape(256, 512))
    wuv = np.ascontiguousarray(inp["l0_w_uv"].transpose(1, 0, 2).reshape(256, 512))
    for c in range(8):
        b, hf = c // 2, c % 2
        xt = np.zeros((D, S), np.float32)
        if hf == 1:
            xt[:, :] = x[b].T
        else:
            xt[:, HALF:] = x[b, :HALF].T
        fl = np.zeros((128, 2), np.float32)
        fl[:, 0] = float(hf)
        fl[:, 1] = (float(hf) - 1.0) * 1.0e30
        m = {"xT": xt, "consts": consts, "flags": fl, "vecs": vecs, "vecs2": vecs2,
             "l0_in_w": inp["l0_in_w"], "l0_out_w": inp["l0_out_w"], "wukT": wukT, "wuv": wuv,
             "l1_in_w": inp["l1_in_w"], "l1_out_w": inp["l1_out_w"]}
        for nm in ("l0_w_gate", "l0_w_up", "l0_w_down", "l1_w_gate", "l1_w_up", "l1_w_down"):
            m[nm] = inp[nm]
        maps.append(m)
    return maps


def kernel(**inputs):
    inp = {k: np.asarray(v) for k, v in inputs.items()}
    nc = build_program()
    maps = make_in_maps(inp)
    res = run_bass_kernel_spmd(nc, maps, core_ids=list(range(8)))
    out = np.zeros((4, S, D), np.float32)
    for c in range(8):
        b, hf = c // 2, c % 2
        out[b, hf * HALF:(hf + 1) * HALF, :] = res.results[c]["yT"].T
    return out
```

```python
import os
import numpy as np
import concourse.bass as bass
import concourse.mybir as mybir
from concourse.bass_utils import run_bass_kernel_spmd
from contextlib import ExitStack

F32 = mybir.dt.float32
BF16 = mybir.dt.bfloat16
AF = mybir.ActivationFunctionType
ALU = mybir.AluOpType
AX = mybir.AxisListType

ENGS = ("sync", "scalar", "vector", "gpsimd", "tensor")
NDMASEM = 8

D = 1024
S = 8192
HALF = 4096
EXT = 4224
EXT0 = S - EXT
TW = 384
NT = EXT // TW
DFF = 2816
NFF = DFF // 128
NEG = -1.0e30


class Buf:
    __slots__ = ("name", "w", "r")

    def __init__(self, name):
        self.name = name
        self.w = {}
        self.r = {}


class T:
    def __init__(self, t, name):
        self.t = t
        self.b = Buf(name)

    def __getitem__(self, idx):
        return self.t[idx]

    def rb(self, c0, c1):
        sub = getattr(self, "sub", None)
        if not sub:
            return [self]
        return [sub[j] for j in range(c0 // 512, (c1 - 1) // 512 + 1) if j in sub] or [self]


class _B:
    def __init__(self, name):
        self.b = Buf(name)


class Prog:
    def __init__(self, nc, es):
        self.nc = nc
        self.es = es
        self.q = {e: [] for e in ENGS}
        self.sems = {}
        self.cnt = {}
        self.waited = {e: {} for e in ENGS}
        self.dma_k = {e: 0 for e in ENGS}
        for e in ("scalar", "vector", "gpsimd", "tensor"):
            self._mksem("E_" + e)
        self.nwaits = 0
        self.nops = 0

    def _mksem(self, key):
        self.sems[key] = self.es.enter_context(self.nc.semaphore(key))
        self.cnt[key] = 0

    def _need(self, eng, toks, key, val):
        if key == "E_tensor" and eng == "tensor":
            return
        if self.waited[eng].get(key, 0) >= val:
            return
        if toks.get(key, 0) < val:
            toks[key] = val

    def op(self, eng, fn, reads=(), writes=(), dma=False, partial=False):
        toks = {}
        for b in reads:
            for k, v in b.w.items():
                self._need(eng, toks, k, v)
        for b in writes:
            for k, v in b.w.items():
                self._need(eng, toks, k, v)
            for k, v in b.r.items():
                self._need(eng, toks, k, v)
        if dma:
            k = self.dma_k[eng]
            self.dma_k[eng] += 1
            key = "D_%s_%d" % (eng, k % NDMASEM)
            if key not in self.sems:
                self._mksem(key)
            prev = self.cnt[key]
            if prev > 0:
                self._need(eng, toks, key, prev)
            self.cnt[key] += 16
            inc = 16
        else:
            key = "E_" + eng
            self.cnt[key] += 1
            inc = 1
        val = self.cnt[key]
        for k2, v2 in toks.items():
            self.waited[eng][k2] = v2
        self.nwaits += len(toks)
        self.nops += 1
        self.q[eng].append((list(toks.items()), fn, key, inc))
        for b in reads:
            b.r[key] = val
        for b in writes:
            if partial:
                b.w[key] = val
            else:
                b.w = {key: val}
                b.r = {}
        return (key, val)

    def barrier(self):
        for e in ENGS:
            toks = {}
            for k, v in self.cnt.items():
                if v > 0:
                    self._need(e, toks, k, v)
            for k2, v2 in toks.items():
                self.waited[e][k2] = v2
            if toks:
                self.q[e].append((list(toks.items()), None, None, 0))

    def emit(self):
        nc = self.nc
        with nc.Block() as block:
            for e in ENGS:
                ops = self.q[e]

                def body(engine, ops=ops):
                    for waits, fn, key, inc in ops:
                        for k, v in waits:
                            engine.wait_ge(self.sems[k], v)
                        if fn is not None:
                            ins = fn(engine)
                            ins.then_inc(self.sems[key], inc)

                getattr(block, e)(body)


class I:
    def __init__(self, name, *a, **kw):
        self.name = name
        self.a = a
        self.kw = kw

    def __call__(self, e):
        return getattr(e, self.name)(*self.a, **self.kw)


class KB:
    def __init__(self, nc, P):
        self.nc = nc
        self.P = P
        self.dq = 0

    def sb(self, es, name, shape, dt):
        self.dq += 1
        name = "sb%d_%s" % (self.dq, name)
        return T(es.enter_context(self.nc.sbuf_tensor(name, list(shape), dt)), name)

    def ps(self, es, name, shape=(128, 512), dt=F32):
        self.dq += 1
        name = "pp%d_%s" % (self.dq, name)
        return T(es.enter_context(self.nc.psum_tensor(name, list(shape), dt)), name)

    def _op(self, eng, fn, r, w, **kw):
        return self.P.op(eng, fn, [x.b for x in r], [x.b for x in w], **kw)

    def V(self, fn, r, w, **kw):
        return self._op("vector", fn, r, w, **kw)

    def A(self, fn, r, w, **kw):
        return self._op("scalar", fn, r, w, **kw)

    def G(self, fn, r, w, **kw):
        return self._op("gpsimd", fn, r, w, **kw)

    def PE(self, fn, r, w, **kw):
        return self._op("tensor", fn, r, w, **kw)

    def DMA(self, out, in_, r, w, eng="sync", partial=True):
        return self._op(eng, I("dma_start", out=out, in_=in_), r, w, dma=True, partial=partial)


class Shared:
    pass


def load_weight(kb, sh, dst, src, kchunks, ncols, gain=None, dcol0=0, act_share=False):
    srcv = src.rearrange("(k p) n -> p k n", p=128)
    if not hasattr(dst, "sub"):
        dst.sub = {}
    c = 0
    while c < ncols:
        cw = min(512 - (dcol0 + c) % 512, ncols - c)
        j = (dcol0 + c) // 512
        if j not in dst.sub:
            dst.sub[j] = _B("w%d" % j)
        sb_ = dst.sub[j]
        for k0 in range(0, kchunks, 4):
            nk = min(4, kchunks - k0)
            st = sh.stage[sh.stage_i % 2]
            sh.stage_i += 1
            stv = st[:, :nk * cw].rearrange("p (k n) -> p k n", k=nk)
            kb.DMA(stv, srcv[:, k0:k0 + nk, c:c + cw], [], [st], partial=False)
            for kk in range(nk):
                k = k0 + kk
                o = dst[:, k, dcol0 + c:dcol0 + c + cw]
                if gain is None and act_share and kk % 2 == 1:
                    kb.A(I("activation", out=o, in_=stv[:, kk, :], func=AF.Copy), [st], [sb_], partial=True)
                elif gain is None:
                    kb.G(I("tensor_scalar", out=o, in0=stv[:, kk, :], scalar1=sh.ident_f[:, 384:385], scalar2=1.0, op0=ALU.mult, op1=ALU.mult),
                         [st, sh.ident_f], [sb_], partial=True)
                else:
                    kb.G(I("tensor_scalar", out=o, in0=stv[:, kk, :], scalar1=gain[:, k:k + 1], scalar2=1.0, op0=ALU.mult, op1=ALU.mult),
                         [st, gain], [sb_], partial=True)
        c += cw


def rmsnorm_fm(kb, sh, xf, W, xn, nk=8, dim=1024.0, psb=None):
    sq = sh.sq
    kb.A(I("activation", out=sq[:, :nk, :W], in_=xf[:, :nk, :W], func=AF.Square), [xf], [sq])
    ps = psb if psb is not None else sh.next_ps()
    for k in range(nk):
        kb.PE(I("matmul", ps[:, :W], lhsT=sh.ones[:, :], rhs=sq[:, k, :W],
                                      start=(k == 0), stop=(k == nk - 1)), [sh.ones, sq], [ps])
    rs = sh.rstd[sh.rstd_i % 2]
    sh.rstd_i += 1
    kb.A(I("activation", out=rs[:, :W], in_=ps[:, :W], func=AF.Sqrt, scale=1.0 / dim, bias=sh.eps[:, 0:1]),
         [ps, sh.eps], [rs])
    kb.V(I("reciprocal", out=rs[:, :W], in_=rs[:, :W]), [rs], [rs])
    if xn is not None:
        for k in range(nk):
            kb.V(I("tensor_tensor", out=xn[:, k, :W], in0=xf[:, k, :W], in1=rs[:, :W], op=ALU.mult),
                 [xf, rs], [xn], partial=(k > 0))
    return rs


def proj(kb, ps, wt, col0, xn, W, nk=8, M=128, k0=0, start=True, stop=True, xk0=0):
    for k in range(nk):
        kb.PE(I("matmul", ps[:M, :W], lhsT=wt[:, k0 + k, col0:col0 + M], rhs=xn[:, xk0 + k, :W],
                                      start=(start and k == 0), stop=(stop and k == nk - 1)), wt.rb(col0, col0 + M) + [xn], [ps])


def build_program(debug=None):
    nc = bass.Bass("TRN2", target_bir_lowering=False)
    dbg = {}

    def din(name, shape, dt=F32):
        return nc.dram_tensor(name, list(shape), dt, kind="ExternalInput").ap()

    def dscr(name, shape, dt):
        return nc.dram_tensor(name, list(shape), dt, kind="Internal").ap()

    xT = din("xT", [D, S])
    consts = din("consts", [128, 1024])
    flags = din("flags", [128, 2])
    vecs = din("vecs", [128, 64])
    l0_in_w = din("l0_in_w", [D, 3928])
    l0_out_w = din("l0_out_w", [1536, D])
    wukT_d = din("wukT", [256, 512])
    wuv_d = din("wuv", [256, 512])
    vecs2 = din("vecs2", [128, 128])
    l1_in_w = din("l1_in_w", [D, 3072])
    l1_out_w = din("l1_out_w", [D, D])
    ffn_w = [(din("l0_w_gate", [D, DFF]), din("l0_w_up", [D, DFF]), din("l0_w_down", [DFF, D])),
             (din("l1_w_gate", [D, DFF]), din("l1_w_up", [D, DFF]), din("l1_w_down", [DFF, D]))]
    yT = nc.dram_tensor("yT", [D, HALF], F32, kind="ExternalOutput").ap()

    xa_d = dscr("xa_d", [D, EXT], F32)
    xb_d = dscr("xb_d", [D, EXT], F32)
    xc_d = dscr("xc_d", [D, EXT], F32)
    h_d = dscr("h_d", [DFF, EXT], BF16)
    DBG = (debug in ("p1a", "p1b", "p2"))

    def dscr2(name, shape, dt):
        if DBG:
            return nc.dram_tensor(name, list(shape), dt, kind="ExternalOutput").ap()
        return dscr(name, shape, dt)
    kT_d = dscr2("kT_d", [512, S], BF16)
    V_d = dscr2("V_d", [8, 128, 64, 65], BF16)
    ikT_d = dscr2("ikT_d", [128, S], BF16)
    qT_d = dscr2("qT_d", [512, EXT], BF16)
    iqT_d = dscr2("iqT_d", [512, EXT], BF16)
    iw_d = dscr2("iw_d", [33, 128, 16], F32)
    mx_d = dscr2("mx_d", [128, 16], F32)
    ybT_d = dscr2("ybT_d", [D, EXT], BF16)
    yaT_d = dscr2("yaT_d", [512, EXT], BF16)

    VC = dict(l0_attn_norm=0, l0_ffn_norm=8, l1_conv_norm=16, l1_ffn_norm=24, final_norm=32,
              l1_conv_w=40)

    with ExitStack() as es:
        P = Prog(nc, es)
        kb = KB(nc, P)
        sh = Shared()
        cst = kb.sb(es, "cst", [128, 1024], F32)
        sh.ident_f = cst
        sh.ones = kb.sb(es, "ones", [128, 128], BF16)
        sh.ident = kb.sb(es, "ident", [128, 128], BF16)
        sh.eps = kb.sb(es, "eps", [128, 1], F32)
        sh.flags = kb.sb(es, "flags", [128, 2], F32)
        sh.vecs = kb.sb(es, "vecs", [128, 64], F32)
        kb.DMA(cst[:, :], consts[:, :], [], [cst], partial=False)
        kb.DMA(sh.flags[:, :], flags[:, :], [], [sh.flags], partial=False)
        kb.DMA(sh.vecs[:, :], vecs[:, :], [], [sh.vecs], partial=False)
        kb.V(I("tensor_copy", out=sh.ones[:, :], in_=cst[:, 384:512]), [cst], [sh.ones])
        kb.V(I("tensor_copy", out=sh.ident[:, :], in_=cst[:, 0:128]), [cst], [sh.ident])
        kb.V(I("memset", sh.eps[:, :], 1e-6), [], [sh.eps])
        pall = es.enter_context(nc.psum_tensor("pall", [128, 8, 512], F32))

        class PB:
            def __init__(self, ap, name):
                self.ap = ap
                self.b = Buf(name)

            def __getitem__(self, idx):
                return self.ap[idx]

            def bf(self):
                return self.ap.bitcast(BF16)

        psums = [PB(pall[:, i, :], "ps%d" % i) for i in range(8)]
        pdbl = [PB(pall[:, 4 + 2 * i:6 + 2 * i, :].rearrange("p a b -> p (a b)"), "pd%d" % i) for i in range(2)]
        sh.ps_i = 0
        sh.nps = 8
        sh.pd_i = 0

        def next_pd():
            p = pdbl[sh.pd_i % 2]
            sh.pd_i += 1
            return p

        def next_ps():
            p = psums[sh.ps_i % sh.nps]
            sh.ps_i += 1
            return p
        sh.next_ps = next_ps

        def gain(name):
            return _GainView(sh.vecs, VC[name])

        class _GainView:
            def __init__(self, t, c):
                self.t = t.t
                self.b = t.b
                self.c = c

            def __getitem__(self, idx):
                p, k = idx
                if isinstance(k, slice):
                    k = slice(k.start + self.c, k.stop + self.c)
                else:
                    k = k + self.c
                return self.t[p, k]

        def common_tiles(pes):
            sh.stage = [kb.sb(pes, "stage%d" % i, [128, 2048], F32) for i in range(2)]
            sh.stage_i = 0
            sh.sq = kb.sb(pes, "sq", [128, 8, 512], BF16)
            sh.rstd = [kb.sb(pes, "rstd%d" % i, [128, 512], F32) for i in range(2)]
            sh.rstd_i = 0

        def xview(d, k=8):
            return d.rearrange("(k p) t -> p k t", p=128)

        def phase_ffn_a(src_d, wg_d, wu_d, gname, pre=None):
            with ExitStack() as pes:
                common_tiles(pes)
                sh.nps = 8
                wg = kb.sb(pes, "wg", [128, 8, DFF], BF16)
                wu = kb.sb(pes, "wu", [128, 8, DFF], BF16)
                g = gain(gname)
                xf = [kb.sb(pes, "xf%d" % i, [128, 8, TW], F32) for i in range(2)]
                xn = [kb.sb(pes, "xn%d" % i, [128, 8, TW], BF16) for i in range(2)]
                hT = [kb.sb(pes, "hT%d" % i, [128, NFF, TW], BF16) for i in range(2)]
                sg = [kb.sb(pes, "sg%d" % i, [128, TW], BF16) for i in range(3)]
                sv = xview(src_d)
                hv = h_d.rearrange("(k p) t -> p k t", p=128)
                sgi = [0]

                def a_pre(i):
                    t0 = i * TW
                    x_, n_ = xf[i % 2], xn[i % 2]
                    kb.DMA(x_[:, :, :], sv[:, :, t0:t0 + TW], [], [x_], partial=False)
                    rmsnorm_fm(kb, sh, x_, TW, n_)

                def a_step(i, n):
                    n_, h_ = xn[i % 2], hT[i % 2]
                    pg, pu = next_ps(), next_ps()
                    proj(kb, pg, wg, n * 128, n_, TW)
                    proj(kb, pu, wu, n * 128, n_, TW)
                    s_ = sg[sgi[0] % 3]
                    sgi[0] += 1
                    kb.A(I("activation", out=s_[:, :], in_=pg[:, :TW], func=AF.Silu), [pg], [s_])
                    kb.V(I("tensor_tensor",
                        out=h_[:, n, :], in0=pu[:, :TW], in1=s_[:, :], op=ALU.mult), [pu, s_], [h_], partial=(n > 0))

                def a_post(i):
                    t0 = i * TW
                    h_ = hT[i % 2]
                    kb.DMA(hv[:, :, t0:t0 + TW], h_[:, :, :], [h_], [], eng="gpsimd")

                a_pre(0)
                a_pre(1)
                for c_ in range(0, DFF, 512):
                    cw_ = min(512, DFF - c_)
                    load_weight(kb, sh, wg, wg_d[:, c_:c_ + cw_], 8, cw_, g, c_)
                    load_weight(kb, sh, wu, wu_d[:, c_:c_ + cw_], 8, cw_, g, c_)
                for n in range(NFF):
                    a_step(0, n)
                    a_step(1, n)
                a_post(0)
                a_post(1)
                for i in range(2, NT):
                    a_pre(i)
                    for n in range(NFF):
                        a_step(i, n)
                    a_post(i)
                P.barrier()

        def phase_ffn_b(src_d, wd_d, dst_d=None, final_gain=None):
            with ExitStack() as pes:
                common_tiles(pes)
                sh.nps = 8
                wd = kb.sb(pes, "wd", [128, NFF, D], BF16)
                xf = [kb.sb(pes, "xf%d" % i, [128, 8, TW], F32) for i in range(2)]
                xo = [kb.sb(pes, "xo%d" % i, [128, 8, TW], F32) for i in range(2)]
                hT = [kb.sb(pes, "hT%d" % i, [128, NFF, TW], BF16) for i in range(2)]
                sv = xview(src_d)
                hv = h_d.rearrange("(k p) t -> p k t", p=128)
                if final_gain is not None:
                    fg = gain(final_gain)
                    yo = [kb.sb(pes, "yo%d" % i, [128, 8, TW], F32) for i in range(2)]
                    yv = xview(yT)
                else:
                    dv = xview(dst_d)
                def b_pre(i):
                    t0 = i * TW
                    x_, h_ = xf[i % 2], hT[i % 2]
                    kb.DMA(x_[:, :, :], sv[:, :, t0:t0 + TW], [], [x_], partial=False)
                    kb.DMA(h_[:, :, :], hv[:, :, t0:t0 + TW], [], [h_], partial=False)

                def b_step(i, n):
                    x_, o_, h_ = xf[i % 2], xo[i % 2], hT[i % 2]
                    ps = next_ps()
                    proj(kb, ps, wd, n * 128, h_, TW, nk=NFF)
                    kb.V(I("tensor_tensor",
                        out=o_[:, n, :], in0=ps[:, :TW], in1=x_[:, n, :], op=ALU.add), [ps, x_], [o_], partial=(n > 0))

                b_pre(0)
                b_pre(1)
                load_weight(kb, sh, wd, wd_d, NFF, D, None, act_share=True)
                for n in range(8):
                    b_step(0, n)
                    b_step(1, n)
                for i in range(NT):
                    t0 = i * TW
                    x_, o_, h_ = xf[i % 2], xo[i % 2], hT[i % 2]
                    if i >= 2:
                        b_pre(i)
                        for n in range(8):
                            b_step(i, n)
                    if final_gain is None:
                        kb.DMA(dv[:, :, t0:t0 + TW], o_[:, :, :], [o_], [], eng="gpsimd")
                    else:
                        rs = rmsnorm_fm(kb, sh, o_, TW, None)
                        y_ = yo[i % 2]
                        for k in range(8):
                            kb.V(I("scalar_tensor_tensor",
                                out=y_[:, k, :], in0=o_[:, k, :], scalar=fg[:, k:k + 1], in1=rs[:, :TW],
                                op0=ALU.mult, op1=ALU.mult), [o_, rs, sh.vecs], [y_], partial=(k > 0))
                        c0 = 128 if i == 0 else 0
                        kb.DMA(yv[:, :, t0 + c0 - 128:t0 + TW - 128], y_[:, :, c0:TW], [y_], [sh.outbuf], eng="gpsimd")
                P.barrier()

        def phase_l1_mixer(src_d, dst_d):
            with ExitStack() as pes:
                common_tiles(pes)
                sh.nps = 8
                wi = kb.sb(pes, "wi", [128, 8, 3072], BF16)
                wo = kb.sb(pes, "wo", [128, 8, D], BF16)
                dg = kb.sb(pes, "dg", [128, 8, 3, 128], BF16)
                cw = VC["l1_conv_w"]
                for n in range(8):
                    for k in range(3):
                        kb.V(I("tensor_scalar",
                            out=dg[:, n, k, :], in0=cst[:, 0:128], scalar1=sh.vecs[:, cw + n * 3 + k:cw + n * 3 + k + 1],
                            scalar2=None, op0=ALU.mult), [cst, sh.vecs], [dg], partial=True)
                xf = [kb.sb(pes, "xf%d" % i, [128, 8, TW], F32) for i in range(2)]
                xn = [kb.sb(pes, "xn%d" % i, [128, 8, TW], BF16) for i in range(2)]
                xo = [kb.sb(pes, "xo%d" % i, [128, 8, TW], F32) for i in range(2)]
                u = [kb.sb(pes, "u%d" % i, [128, 8, TW + 2], BF16) for i in range(2)]
                yb = [kb.sb(pes, "yb%d" % i, [128, 8, TW], BF16) for i in range(2)]
                tmp = [kb.sb(pes, "tmp%d" % i, [128, TW], F32) for i in range(3)]
                sv = xview(src_d)
                dv = xview(dst_d)
                kb.V(I("memset", u[0][:, :, 0:2], 0.0), [], [u[0]])
                for i in range(2):
                    kb.DMA(xf[i][:, :, :], sv[:, :, i * TW:(i + 1) * TW], [], [xf[i]], partial=False)
                gl1 = gain("l1_conv_norm")
                load_weight(kb, sh, wi, l1_in_w[:, 1024:3072], 8, 2048, gl1, 1024)
                load_weight(kb, sh, wi, l1_in_w[:, 0:1024], 8, 1024, gl1, 0)
                load_weight(kb, sh, wo, l1_out_w, 8, D, None)
                for i in range(NT):
                    t0 = i * TW
                    x_, n_, o_, u_, y_ = xf[i % 2], xn[i % 2], xo[i % 2], u[i % 2], yb[i % 2]
                    up = u[(i + 1) % 2]
                    if i >= 2:
                        kb.DMA(x_[:, :, :], sv[:, :, t0:t0 + TW], [], [x_], partial=False)
                    rmsnorm_fm(kb, sh, x_, TW, n_)
                    if i > 0:
                        kb.V(I("tensor_copy", out=u_[:, :, 0:2], in_=up[:, :, TW:TW + 2]), [up], [u_])
                    for n in range(8):
                        pc, pv = next_ps(), next_ps()
                        proj(kb, pc, wi, 1024 + n * 128, n_, TW)
                        proj(kb, pv, wi, 2048 + n * 128, n_, TW)
                        t_ = tmp[n % 3]
                        kb.A(I("activation", out=t_[:, :], in_=pc[:, :TW], func=AF.Copy), [pc], [t_])
                        kb.V(I("tensor_tensor",
                            out=u_[:, n, 2:TW + 2], in0=pv[:, :TW], in1=t_[:, :], op=ALU.mult), [pv, t_], [u_], partial=True)
                    if i == 0:
                        kb.V(I("tensor_scalar",
                            out=u_[:, :, 2:130], in0=u_[:, :, 2:130], scalar1=sh.flags[:, 0:1], scalar2=None,
                            op0=ALU.mult), [u_, sh.flags], [u_])
                    for n in range(8):
                        pcv, pb = next_ps(), next_ps()
                        for k in range(3):
                            kb.PE(I("matmul",
                                pcv[:, :TW], lhsT=dg[:, n, k, :], rhs=u_[:, n, k:k + TW], start=(k == 0), stop=(k == 2)),
                                [dg, u_], [pcv])
                        proj(kb, pb, wi, n * 128, n_, TW)
                        t_ = tmp[n % 3]
                        kb.A(I("activation", out=t_[:, :], in_=pb[:, :TW], func=AF.Copy), [pb], [t_])
                        kb.V(I("tensor_tensor",
                            out=y_[:, n, :], in0=pcv[:, :TW], in1=t_[:, :], op=ALU.mult), [pcv, t_], [y_], partial=(n > 0))
                    for n in range(8):
                        ps = next_ps()
                        proj(kb, ps, wo, n * 128, y_, TW)
                        kb.V(I("tensor_tensor",
                            out=o_[:, n, :], in0=ps[:, :TW], in1=x_[:, n, :], op=ALU.add), [ps, x_], [o_], partial=(n > 0))
                    kb.DMA(dv[:, :, t0:t0 + TW], o_[:, :, :], [o_], [], eng="gpsimd")
                P.barrier()

        tiles_all = [(0, 128, None)] + [(128 + 384 * i, 384, None) for i in range(10)] + \
                    [(EXT0 + TW * i, TW, TW * i) for i in range(NT)]
        sh.v2 = kb.sb(es, "v2", [128, 128], F32)
        kb.DMA(sh.v2[:, :], vecs2[:, :], [], [sh.v2], partial=False)
        sh.sel = kb.sb(es, "sel", [128, 2, 128], BF16)
        kb.V(I("tensor_copy", out=sh.sel[:, :, :], in_=cst[:, 512:768].rearrange("p (a b) -> p a b", a=2)), [cst], [sh.sel])
        sh.kmax = kb.sb(es, "kmax", [128, 8], F32)
        sh.qmax = kb.sb(es, "qmax", [128, 8], F32)
        kb.V(I("memset", sh.kmax[:, :], 0.0), [], [sh.kmax])
        kb.V(I("memset", sh.qmax[:, :], 0.0), [], [sh.qmax])

        def phase_1a():
            LVL = int(os.environ.get("LVL", "9"))
            with ExitStack() as pes:
                common_tiles(pes)
                sh.nps = 8
                Wa = kb.sb(pes, "Wa", [128, 8, 1416], BF16)
                g = gain("l0_attn_norm")
                wuk = kb.sb(pes, "wuk", [128, 2, 512], BF16)
                wuv = kb.sb(pes, "wuv", [128, 2, 512], BF16)
                xf = [kb.sb(pes, "xf%d" % i, [128, 8, TW], F32) for i in range(2)]
                xn = [kb.sb(pes, "xn%d" % i, [128, 8, TW], BF16) for i in range(2)]
                sqk = kb.sb(pes, "sqk", [128, 2, TW], BF16)
                ckvn = kb.sb(pes, "ckvn", [128, 2, TW], BF16)
                sqq = [kb.sb(pes, "sqq%d" % i, [128, TW], BF16) for i in range(2)]
                kT_sb = [kb.sb(pes, "kT%d" % i, [128, 4, TW], BF16) for i in range(2)]
                qT_sb = [kb.sb(pes, "qT%d" % i, [128, 4, TW], BF16) for i in range(2)]
                iq_sb = [kb.sb(pes, "iq%d" % i, [128, 4, TW], BF16) for i in range(2)]
                V_sb = [kb.sb(pes, "V%d" % i, [128, 3, 8, 65], BF16) for i in range(2)]
                ik_sb = [kb.sb(pes, "ik%d" % i, [128, TW], BF16) for i in range(2)]
                iw_sb = [kb.sb(pes, "iw%d" % i, [128, 3, 16], F32) for i in range(2)]
                mt = kb.sb(pes, "mt", [128, 8], F32)
                for v_ in V_sb:
                    kb.V(I("memset", v_[:, :, :, :], 1.0), [], [v_])
                kTv = kT_d.rearrange("(j p) t -> p j t", p=128)
                qTv = qT_d.rearrange("(j p) t -> p j t", p=128)
                iqTv = iqT_d.rearrange("(j p) t -> p j t", p=128)
                Vv = V_d.rearrange("h p b d -> p b h d")
                iwv = iw_d.rearrange("b p f -> p b f")
                xv = xview(xT)

                def norms(pt, W, dstmax, j):
                    s_ = sqq[j % 2]
                    kb.A(I("activation", out=s_[:, :W], in_=pt[:, :W], func=AF.Square), [pt], [s_])
                    for hh in range(2):
                        pn = next_ps()
                        kb.PE(I("matmul", pn[:, :W], lhsT=sh.sel[:, hh, :], rhs=s_[:, :W], start=True, stop=True),
                              [sh.sel, s_], [pn])
                        kb.V(I("tensor_reduce", out=mt[:, 2 * j + hh:2 * j + hh + 1], in_=pn[:, :W], op=ALU.max, axis=AX.X),
                             [pn], [mt], partial=True)
                    if j == 3:
                        kb.V(I("tensor_tensor", out=dstmax[:, :], in0=dstmax[:, :], in1=mt[:, :], op=ALU.max), [mt, dstmax], [dstmax])

                for ti in range(2):
                    t0, W, ec = tiles_all[ti]
                    kb.DMA(xf[ti][:, :, :W], xv[:, :, t0:t0 + W], [], [xf[ti]], partial=False)
                load_weight(kb, sh, Wa, l0_in_w[:, 512:768], 8, 256, g, 0)
                load_weight(kb, sh, Wa, l0_in_w[:, 1280:1344], 8, 64, g, 256)
                load_weight(kb, sh, Wa, l0_in_w[:, 1280:1344], 8, 64, g, 320)
                load_weight(kb, sh, Wa, l0_in_w[:, 0:512], 8, 512, g, 384)
                load_weight(kb, sh, Wa, l0_in_w[:, 768:1280], 8, 512, g, 896)
                load_weight(kb, sh, Wa, l0_in_w[:, 1344:1352], 8, 8, g, 1408)
                load_weight(kb, sh, wuk, wukT_d, 2, 512, None)
                load_weight(kb, sh, wuv, wuv_d, 2, 512, None)
                for ti, (t0, W, ec) in enumerate(tiles_all):
                    nblk = W // 128
                    ci0 = t0 // 128
                    x_, n_ = xf[ti % 2], xn[ti % 2]
                    if ti >= 2:
                        kb.DMA(x_[:, :, :W], xv[:, :, t0:t0 + W], [], [x_], partial=False)
                    rmsnorm_fm(kb, sh, x_, W, n_)
                    pk = [next_ps(), next_ps()]
                    for c in range(2):
                        proj(kb, pk[c], Wa, c * 128, n_, W)
                        kb.A(I("activation", out=sqk[:, c, :W], in_=pk[c][:, :W], func=AF.Square), [pk[c]], [sqk], partial=(c > 0))
                    pss = next_ps()
                    for c in range(2):
                        kb.PE(I("matmul", pss[:, :W], lhsT=sh.ones[:, :], rhs=sqk[:, c, :W], start=(c == 0), stop=(c == 1)),
                              [sh.ones, sqk], [pss])
                    rs = sh.rstd[sh.rstd_i % 2]
                    sh.rstd_i += 1
                    kb.A(I("activation", out=rs[:, :W], in_=pss[:, :W], func=AF.Sqrt, scale=1.0 / 256.0, bias=sh.eps[:, 0:1]),
                         [pss, sh.eps], [rs])
                    kb.V(I("reciprocal", out=rs[:, :W], in_=rs[:, :W]), [rs], [rs])
                    for c in range(2):
                        kb.V(I("scalar_tensor_tensor", out=ckvn[:, c, :W], in0=pk[c][:, :W], scalar=sh.v2[:, c:c + 1], in1=rs[:, :W],
                                                                    op0=ALU.mult, op1=ALU.mult), [pk[c], sh.v2, rs], [ckvn], partial=(c > 0))
                    if LVL < 2:
                        continue
                    k_ = kT_sb[ti % 2]
                    for j in range(4):
                        pt = next_ps()
                        for c in range(2):
                            kb.PE(I("matmul", pt[:, :W], lhsT=wuk[:, c, j * 128:(j + 1) * 128], rhs=ckvn[:, c, :W],
                                                                      start=(c == 0), stop=(c == 1)), wuk.rb(j * 128, (j + 1) * 128) + [ckvn], [pt])
                        kb.A(I("activation", out=k_[:, j, :W], in_=pt[:, :W], func=AF.Copy), [pt], [k_], partial=(j > 0))
                        norms(pt, W, sh.kmax, j)
                    kb.DMA(kTv[:, :, t0:t0 + W], k_[:, :, :W], [k_], [], eng="gpsimd")
                    if LVL < 3:
                        continue
                    v_ = V_sb[ti % 2]
                    for blk in range(nblk):
                        pv = next_ps()
                        for c in range(2):
                            kb.PE(I("matmul", pv[:, :512], lhsT=ckvn[:, c, blk * 128:(blk + 1) * 128], rhs=wuv[:, c, :],
                                                                          start=(c == 0), stop=(c == 1)), wuv.rb(0, 512) + [ckvn], [pv])
                        kb.A(I("activation", out=v_[:, blk, :, 0:64], in_=pv[:, :512].rearrange("p (h d) -> p h d", h=8),
                                                                    func=AF.Copy), [pv], [v_], partial=(blk > 0))
                    for h in range(8):
                        kb.DMA(V_d[h, :, ci0:ci0 + nblk, :], v_[:, :nblk, h, :], [v_], [], eng="gpsimd")
                    if LVL < 4:
                        continue
                    pik = next_ps()
                    proj(kb, pik, Wa, 256, n_, W)
                    i_ = ik_sb[ti % 2]
                    kb.A(I("activation", out=i_[:, :W], in_=pik[:, :W], func=AF.Copy), [pik], [i_])
                    kb.DMA(ikT_d[:, t0:t0 + W], i_[:, :W], [i_], [], eng="gpsimd")
                    if ec is None or LVL < 5:
                        continue
                    q_, iq_, w_ = qT_sb[ti % 2], iq_sb[ti % 2], iw_sb[ti % 2]
                    for j in range(4):
                        pq = next_ps()
                        proj(kb, pq, Wa, 384 + j * 128, n_, W)
                        kb.A(I("activation", out=q_[:, j, :W], in_=pq[:, :W], func=AF.Copy), [pq], [q_], partial=(j > 0))
                        norms(pq, W, sh.qmax, j)
                    for j in range(4):
                        pq = next_ps()
                        proj(kb, pq, Wa, 896 + j * 128, n_, W)
                        kb.A(I("activation", out=iq_[:, j, :W], in_=pq[:, :W], func=AF.Copy), [pq], [iq_], partial=(j > 0))
                    if LVL < 6:
                        continue
                    pw = next_ps()
                    for blk in range(nblk):
                        for k in range(8):
                            kb.PE(I("matmul", pw[:, blk * 8:blk * 8 + 8], lhsT=n_[:, k, blk * 128:(blk + 1) * 128],
                                                                   rhs=Wa[:, k, 1408:1416], start=(k == 0), stop=(k == 7)), Wa.rb(1408, 1416) + [n_], [pw])
                    pwv = pw[:, 0:nblk * 8].rearrange("p (b h) -> p b h", h=8)
                    kb.A(I("activation", out=w_[:, :nblk, 0:8], in_=pwv, func=AF.Abs), [pw], [w_])
                    kb.A(I("activation", out=w_[:, :nblk, 8:16], in_=pwv, func=AF.Sign), [pw], [w_], partial=True)
                    kb.DMA(qTv[:, :, ec:ec + W], q_[:, :, :W], [q_], [], eng="gpsimd")
                    kb.DMA(iqTv[:, :, ec:ec + W], iq_[:, :, :W], [iq_], [], eng="gpsimd")
                    kb.DMA(iwv[:, ec // 128:ec // 128 + nblk, :], w_[:, :nblk, :], [w_], [], eng="gpsimd")
                if DBG:
                    mxs = kb.sb(pes, "mxs", [128, 16], F32)
                    kb.V(I("tensor_copy", out=mxs[:, 0:8], in_=sh.kmax[:, :]), [sh.kmax], [mxs])
                    kb.V(I("tensor_copy", out=mxs[:, 8:16], in_=sh.qmax[:, :]), [sh.qmax], [mxs], partial=True)
                    kb.DMA(mx_d[:, :], mxs[:, :], [mxs], [])
                P.barrier()

        def phase_1b():
            with ExitStack() as pes:
                common_tiles(pes)
                sh.nps = 4
                v2 = sh.v2
                Wb = kb.sb(pes, "Wb", [128, 8, 2576], BF16)
                g = gain("l0_attn_norm")
                dgc = kb.sb(pes, "dgc", [128, 12, 4, 128], BF16)
                for n in range(12):
                    for k in range(4):
                        kb.V(I("tensor_scalar", out=dgc[:, n, k, :], in0=cst[:, 0:128], scalar1=v2[:, 2 + n * 4 + k:3 + n * 4 + k],
                               scalar2=None, op0=ALU.mult), [cst, v2], [dgc], partial=True)
                Aneg = kb.sb(pes, "Aneg", [128, 16], F32)
                kb.A(I("activation", out=Aneg[:, :], in_=v2[:, 86:102], func=AF.Exp), [v2], [Aneg])
                kb.V(I("tensor_scalar", out=Aneg[:, :], in0=Aneg[:, :], scalar1=-1.0, scalar2=None, op0=ALU.mult), [Aneg], [Aneg])
                xf = kb.sb(pes, "xf", [128, 8, TW], F32)
                xn = [kb.sb(pes, "xn%d" % i, [128, 8, TW], BF16) for i in range(2)]
                xr = kb.sb(pes, "xr", [128, 12, TW + 3], BF16)
                cr = kb.sb(pes, "cr", [128, 12, 3], BF16)
                xc = kb.sb(pes, "xc", [128, 12, TW], BF16)
                dts = kb.sb(pes, "dts", [128, 3, 16], F32)
                ets = kb.sb(pes, "ets", [128, 3, 16], F32)
                a_sb = kb.sb(pes, "a_sb", [128, 3, 16], F32)
                Sst = kb.sb(pes, "Sst", [128, 1024], F32)
                S_bf = kb.sb(pes, "S_bf", [128, 1024], BF16)
                tmpS = kb.sb(pes, "tmpS", [128, 1024], F32)
                acs = kb.sb(pes, "acs", [128, 16], F32)
                Ee = kb.sb(pes, "Ee", [128, 16], F32)
                dd = kb.sb(pes, "dd", [128, 16], F32)
                dend = kb.sb(pes, "dend", [128, 16], F32)
                cdec = kb.sb(pes, "cdec", [128, 16], F32)
                w1 = kb.sb(pes, "w1", [128, 16], F32)
                xs_sb = kb.sb(pes, "xs_sb", [128, 1024], BF16)
                xdd = kb.sb(pes, "xdd", [128, 1024], BF16)
                xdt = kb.sb(pes, "xdt", [128, 1024], BF16)
                B_sb = kb.sb(pes, "B_sb", [128, 2, 128], BF16)
                cb_sb = kb.sb(pes, "cb_sb", [128, 2, 128], BF16)
                zs = kb.sb(pes, "zs", [128, 1024], BF16)
                seg = [kb.sb(pes, "seg%d" % i, [128, 4, 128], F32) for i in range(2)]
                dec = kb.sb(pes, "dec", [128, 16, 128], BF16)
                MT = kb.sb(pes, "MT", [128, 16, 128], BF16)
                t1 = kb.sb(pes, "t1", [128, 1024], F32)
                t2 = kb.sb(pes, "t2", [128, 1024], F32)
                t3 = kb.sb(pes, "t3", [128, 1024], F32)
                yn = kb.sb(pes, "yn", [128, 1024], BF16)
                junk = kb.sb(pes, "junk", [128, 512], BF16)
                ssq = kb.sb(pes, "ssq", [128, 2], F32)
                rsd = kb.sb(pes, "rsd", [128, 2], F32)
                ybT_sb = [kb.sb(pes, "ybT%d" % i, [128, 8, 128], BF16) for i in range(2)]
                kb.V(I("memset", cr[:, :, :], 0.0), [], [cr])
                kb.V(I("memset", Sst[:, :], 0.0), [], [Sst])
                xv = xview(xT)
                ybv = ybT_d.rearrange("(k p) t -> p k t", p=128)
                tri = cst[:, 128:256]
                negtri = cst[:, 256:384]
                onesf = cst[:, 384:512]
                identf = cst[:, 0:128]

                def b3(ap, n, m):
                    return ap.unsqueeze(2).to_broadcast([128, n, m])

                kb.DMA(xf[:, :, :tiles_all[0][1]], xv[:, :, 0:tiles_all[0][1]], [], [xf], partial=False)
                load_weight(kb, sh, Wb, l0_in_w[:, 2376:3928], 8, 1552, g, 0)
                load_weight(kb, sh, Wb, l0_in_w[:, 1352:2376], 8, 1024, g, 1552)
                for ti, (t0, W, ec) in enumerate(tiles_all):
                    nblk = W // 128
                    n_ = xn[ti % 2]
                    if ti >= 1:
                        kb.DMA(xf[:, :, :W], xv[:, :, t0:t0 + W], [], [xf], partial=False)
                    rmsnorm_fm(kb, sh, xf, W, n_)
                    kb.V(I("tensor_copy", out=xr[:, :, 0:3], in_=cr[:, :, :]), [cr], [xr])
                    for n in range(12):
                        px = next_ps()
                        proj(kb, px, Wb, n * 128, n_, W)
                        kb.A(I("activation", out=xr[:, n, 3:W + 3], in_=px[:, :W], func=AF.Copy), [px], [xr], partial=True)
                    kb.V(I("tensor_copy", out=cr[:, :, :], in_=xr[:, :, W:W + 3]), [xr], [cr])
                    for n in range(12):
                        pc = next_ps()
                        for k in range(4):
                            kb.PE(I("matmul", pc[:, :W], lhsT=dgc[:, n, k, :], rhs=xr[:, n, k:k + W], start=(k == 0), stop=(k == 3)),
                                  [dgc, xr], [pc])
                        kb.A(I("activation", out=xc[:, n, :W], in_=pc[:, :W], func=AF.Silu, bias=v2[:, 50 + n:51 + n]), [pc, v2], [xc],
                             partial=(n > 0))
                    pdt = next_ps()
                    for blk in range(nblk):
                        for k in range(8):
                            kb.PE(I("matmul", pdt[:, blk * 16:blk * 16 + 16], lhsT=n_[:, k, blk * 128:(blk + 1) * 128],
                                    rhs=Wb[:, k, 1536:1552], start=(k == 0), stop=(k == 7)), Wb.rb(1536, 1552) + [n_], [pdt])
                    kb.V(I("tensor_tensor", out=dts[:, :nblk, :], in0=pdt[:, 0:nblk * 16].rearrange("p (b h) -> p b h", h=16),
                           in1=v2[:, 70:86].unsqueeze(1).to_broadcast([128, nblk, 16]), op=ALU.add), [pdt, v2], [dts])
                    kb.A(I("activation", out=ets[:, :nblk, :], in_=dts[:, :nblk, :], func=AF.Exp), [dts], [ets])
                    kb.A(I("activation", out=dts[:, :nblk, :], in_=ets[:, :nblk, :], func=AF.Ln, bias=1.0), [ets], [dts])
                    kb.V(I("tensor_tensor", out=a_sb[:, :nblk, :], in0=dts[:, :nblk, :],
                           in1=Aneg[:, :].unsqueeze(1).to_broadcast([128, nblk, 16]), op=ALU.mult), [dts, Aneg], [a_sb])
                    for blk in range(nblk):
                        ci = t0 // 128 + blk
                        c0 = blk * 128
                        full = ci >= 31
                        if ci == 32:
                            kb.V(I("tensor_scalar", out=Sst[:, :], in0=Sst[:, :], scalar1=sh.flags[:, 0:1], scalar2=None, op0=ALU.mult),
                                 [Sst, sh.flags], [Sst])
                        pT = next_ps()
                        pTb = pT.bf()
                        for n in range(8):
                            kb.PE(I("transpose", out=pTb[:, n * 128:(n + 1) * 128], in_=xc[:, n, c0:c0 + 128], identity=sh.ident[:, :]),
                                  [xc, sh.ident], [pT])
                        pT2 = next_ps()
                        pT2b = pT2.bf()
                        for gg in range(2):
                            kb.PE(I("transpose", out=pT2b[:, gg * 128:(gg + 1) * 128], in_=xc[:, 8 + gg, c0:c0 + 128], identity=sh.ident[:, :]),
                                  [xc, sh.ident], [pT2])
                        pcs = next_ps()
                        kb.PE(I("matmul", pcs[:, 0:16], lhsT=tri, rhs=a_sb[:, blk, :], start=True, stop=True), [cst, a_sb], [pcs])
                        kb.PE(I("matmul", pcs[:, 16:32], lhsT=onesf, rhs=a_sb[:, blk, :], start=True, stop=True), [cst, a_sb], [pcs])
                        kb.A(I("activation", out=acs[:, :], in_=pcs[:, 0:16], func=AF.Copy), [pcs], [acs])
                        if full:
                            kb.A(I("activation", out=Ee[:, :], in_=pcs[:, 0:16], func=AF.Exp), [pcs], [Ee])
                        kb.V(I("tensor_tensor", out=dd[:, :], in0=pcs[:, 16:32], in1=acs[:, :], op=ALU.subtract), [pcs, acs], [dd])
                        kb.A(I("activation", out=dend[:, :], in_=dd[:, :], func=AF.Exp), [dd], [dend])
                        kb.A(I("activation", out=cdec[:, :], in_=pcs[:, 16:32], func=AF.Exp), [pcs], [cdec])
                        kb.V(I("tensor_tensor", out=w1[:, :], in0=dts[:, blk, :], in1=dend[:, :], op=ALU.mult), [dts, dend], [w1])
                        kb.A(I("activation", out=xs_sb[:, :], in_=pTb[:, :], func=AF.Copy), [pT], [xs_sb])
                        kb.G(I("tensor_tensor", out=xdd[:, :].rearrange("p (h d) -> p h d", h=16),
                               in0=xs_sb[:, :].rearrange("p (h d) -> p h d", h=16), in1=b3(w1[:, :], 16, 64), op=ALU.mult),
                             [xs_sb, w1], [xdd])
                        kb.A(I("activation", out=B_sb[:, :, :], in_=pT2b[:, 0:256].rearrange("p (g n) -> p g n", g=2), func=AF.Copy),
                             [pT2], [B_sb])
                        pst = next_pd()
                        for gg in range(2):
                            kb.PE(I("matmul", pst[:, gg * 512:(gg + 1) * 512], lhsT=B_sb[:, gg, :], rhs=xdd[:, gg * 512:(gg + 1) * 512],
                                    start=True, stop=True), [B_sb, xdd], [pst])
                        if full:
                            kb.G(I("tensor_copy", out=S_bf[:, :], in_=Sst[:, :]), [Sst], [S_bf])
                        kb.G(I("tensor_tensor", out=tmpS[:, :].rearrange("p (h d) -> p h d", h=16),
                               in0=Sst[:, :].rearrange("p (h d) -> p h d", h=16), in1=b3(cdec[:, :], 16, 64), op=ALU.mult),
                             [Sst, cdec], [tmpS])
                        kb.V(I("tensor_tensor", out=Sst[:, :], in0=pst[:, :], in1=tmpS[:, :], op=ALU.add), [pst, tmpS], [Sst])
                        if not full:
                            continue
                        ecol = (ci - 31) * 128
                        pz = next_pd()
                        for jz in range(2):
                            for k in range(8):
                                kb.PE(I("matmul", pz[:, jz * 512:(jz + 1) * 512], lhsT=n_[:, k, c0:c0 + 128],
                                        rhs=Wb[:, k, 1552 + jz * 512:1552 + (jz + 1) * 512], start=(k == 0), stop=(k == 7)), Wb.rb(1552 + jz * 512, 1552 + (jz + 1) * 512) + [n_], [pz])
                        kb.A(I("activation", out=zs[:, :], in_=pz[:, :], func=AF.Silu), [pz], [zs])
                        pyo = next_pd()
                        for gg in range(2):
                            kb.PE(I("matmul", pyo[:, gg * 512:(gg + 1) * 512], lhsT=xc[:, 10 + gg, c0:c0 + 128],
                                    rhs=S_bf[:, gg * 512:(gg + 1) * 512], start=True, stop=True), [xc, S_bf], [pyo])
                        kb.V(I("tensor_tensor", out=t1[:, :].rearrange("p (h d) -> p h d", h=16),
                               in0=pyo[:, :].rearrange("p (h d) -> p h d", h=16), in1=b3(Ee[:, :], 16, 64), op=ALU.mult), [pyo, Ee], [t1])
                        pcb = next_ps()
                        for gg in range(2):
                            kb.PE(I("matmul", pcb[:, gg * 128:(gg + 1) * 128], lhsT=xc[:, 8 + gg, c0:c0 + 128], rhs=xc[:, 10 + gg, c0:c0 + 128],
                                    start=True, stop=True), [xc], [pcb])
                        kb.A(I("activation", out=cb_sb[:, :, :], in_=pcb[:, 0:256].rearrange("p (g n) -> p g n", g=2), func=AF.Copy),
                             [pcb], [cb_sb])
                        kb.G(I("tensor_tensor", out=xdt[:, :].rearrange("p (h d) -> p h d", h=16),
                               in0=xs_sb[:, :].rearrange("p (h d) -> p h d", h=16), in1=b3(dts[:, blk, :], 16, 64), op=ALU.mult),
                             [xs_sb, dts], [xdt])
                        for hq in range(4):
                            pR = next_ps()
                            for hh in range(4):
                                h = hq * 4 + hh
                                kb.PE(I("matmul", pR[:, hh * 128:(hh + 1) * 128], lhsT=a_sb[:, blk, h:h + 1].to_broadcast([128, 128]), rhs=tri,
                                        start=True, stop=False), [a_sb, cst], [pR])
                                kb.PE(I("matmul", pR[:, hh * 128:(hh + 1) * 128], lhsT=identf, rhs=negtri, start=False, stop=True), [cst], [pR])
                            sg_ = seg[hq % 2]
                            kb.V(I("tensor_tensor", out=sg_[:, :, :], in0=pR[:, :].rearrange("p (h l) -> p h l", h=4),
                                   in1=b3(acs[:, hq * 4:hq * 4 + 4], 4, 128), op=ALU.subtract), [pR, acs], [sg_])
                            kb.A(I("activation", out=dec[:, hq * 4:hq * 4 + 4, :], in_=sg_[:, :, :], func=AF.Exp), [sg_], [dec], partial=(hq > 0))
                        for gg in range(2):
                            kb.G(I("tensor_tensor", out=MT[:, gg * 8:(gg + 1) * 8, :], in0=dec[:, gg * 8:(gg + 1) * 8, :],
                                   in1=cb_sb[:, gg, :].unsqueeze(1).to_broadcast([128, 8, 128]), op=ALU.mult), [dec, cb_sb], [MT], partial=(gg > 0))
                        pyd = next_pd()
                        for h in range(16):
                            kb.PE(I("matmul", pyd[:, h * 64:(h + 1) * 64], lhsT=MT[:, h, :], rhs=xdt[:, h * 64:(h + 1) * 64], start=True, stop=True),
                                  [MT, xdt], [pyd])
                        kb.V(I("tensor_tensor", out=t2[:, :], in0=pyd[:, :], in1=t1[:, :], op=ALU.add), [pyd, t1], [t2])
                        kb.G(I("tensor_tensor", out=t3[:, :].rearrange("p (h d) -> p h d", h=16),
                               in0=xs_sb[:, :].rearrange("p (h d) -> p h d", h=16), in1=b3(v2[:, 102:118], 16, 64), op=ALU.mult), [xs_sb, v2], [t3])
                        kb.G(I("tensor_tensor", out=t3[:, :], in0=t3[:, :], in1=t2[:, :], op=ALU.add), [t3, t2], [t3])
                        kb.G(I("tensor_tensor", out=t3[:, :], in0=t3[:, :], in1=zs[:, :], op=ALU.mult), [t3, zs], [t3])
                        for gg in range(2):
                            kb.A(I("activation", out=junk[:, :], in_=t3[:, gg * 512:(gg + 1) * 512], func=AF.Square, accum_out=ssq[:, gg:gg + 1]),
                                 [t3], [junk, ssq], partial=(gg > 0))
                        kb.A(I("activation", out=rsd[:, :], in_=ssq[:, :], func=AF.Sqrt, scale=1.0 / 512.0, bias=sh.eps[:, 0:1]), [ssq, sh.eps], [rsd])
                        kb.V(I("reciprocal", out=rsd[:, :], in_=rsd[:, :]), [rsd], [rsd])
                        for gg in range(2):
                            kb.V(I("tensor_scalar", out=yn[:, gg * 512:(gg + 1) * 512], in0=t3[:, gg * 512:(gg + 1) * 512], scalar1=rsd[:, gg:gg + 1],
                                   scalar2=None, op0=ALU.mult), [t3, rsd], [yn], partial=(gg > 0))
                        pY = next_ps()
                        pYb = pY.bf()
                        for n in range(8):
                            kb.PE(I("transpose", out=pYb[:, n * 128:(n + 1) * 128], in_=yn[:, n * 128:(n + 1) * 128], identity=sh.ident[:, :]),
                                  [yn, sh.ident], [pY])
                        yb_ = ybT_sb[ci % 2]
                        kb.V(I("tensor_tensor", out=yb_[:, :, :], in0=pYb[:, :].rearrange("p (n l) -> p n l", n=8),
                               in1=b3(v2[:, 62:70], 8, 128), op=ALU.mult), [pY, v2], [yb_])
                        kb.DMA(ybv[:, :, ecol:ecol + 128], yb_[:, :, :], [yb_], [], eng="gpsimd")
                P.barrier()

        NIT = 16
        FP8 = mybir.dt.float8e4

        def phase_2():
            with ExitStack() as pes:
                sh.nps = 8
                flags_ = sh.flags
                P2NT = int(os.environ.get("P2NT", NT))
                ikT2 = kb.sb(pes, "ikT2", [128, S], BF16)
                kb.DMA(ikT2[:, :], ikT_d[:, :], [], [ikT2], partial=False)
                negM = kb.sb(pes, "negM", [128, 8], F32)
                kb.V(I("tensor_tensor", out=negM[:, :], in0=sh.qmax[:, :], in1=sh.kmax[:, :], op=ALU.add), [sh.qmax, sh.kmax], [negM])
                kb.V(I("tensor_scalar", out=negM[:, :], in0=negM[:, :], scalar1=-1.0 / 16.0, scalar2=None, op0=ALU.mult), [negM], [negM])
                scs = [kb.sb(pes, "sc%d" % i, [128, S], F32) for i in range(2)]
                for t_ in scs:
                    t_.subs = [_B("scsub%d" % j) for j in range(16)]
                mk = kb.sb(pes, "mk", [128, S], BF16)
                maskT = [kb.sb(pes, "maskT%d" % i, [128, 64, TW], FP8) for i in range(2)]
                iq_sb = [kb.sb(pes, "iq_sb%d" % i, [128, 4, 2, TW], BF16) for i in range(2)]
                iw_sb = [kb.sb(pes, "iw_sb%d" % i, [128, 3, 16], F32) for i in range(2)]
                qT_sb = [kb.sb(pes, "qT_sb%d" % i, [128, 8, TW], BF16) for i in range(1)]
                Dg = kb.sb(pes, "Dg", [128, 8, 128], BF16)
                rl = [kb.sb(pes, "rl%d" % i, [128, 512], BF16) for i in range(4)]
                amx = kb.sb(pes, "amx", [128, 16], F32)
                smax = kb.sb(pes, "smax", [128, 1], F32)
                lo = kb.sb(pes, "lo", [128, 1], F32)
                mid = kb.sb(pes, "mid", [128, 1], F32)
                cnt = kb.sb(pes, "cnt", [128, 1], F32)
                ge = kb.sb(pes, "ge", [128, 1], F32)
                wall = kb.sb(pes, "wall", [128, NIT], F32)
                kc = [kb.sb(pes, "kc%d" % i, [128, 2048], BF16) for i in range(2)]
                for t_ in iq_sb:
                    kb.G(I("memset", t_[:, :, :, :], 0.0), [], [t_])
                for t_ in qT_sb + kc:
                    kb.G(I("memset", t_[64:128], 0.0), [], [t_])
                vc = [kb.sb(pes, "vc%d" % i, [128, 16, 65], BF16) for i in range(2)]
                pe_ = [kb.sb(pes, "pe%d" % i, [128, TW], BF16) for i in range(4)]
                o_sb = [kb.sb(pes, "o_sb%d" % i, [65, TW], F32) for i in range(2)]
                rden = [kb.sb(pes, "rden%d" % i, [65, TW], F32) for i in range(2)]
                rb_sb = [kb.sb(pes, "rb%d" % i, [64, TW], F32) for i in range(2)]
                ya_sb = [kb.sb(pes, "ya%d" % i, [64, 8, TW], BF16) for i in range(1)]
                negI = kb.sb(pes, "negI", [128, 128], BF16)
                kb.V(I("tensor_scalar", out=negI[:, :], in0=cst[:, 0:128], scalar1=-4096.0, scalar2=None, op0=ALU.mult), [cst], [negI])
                negtriT = cst[:, 768:896]
                pow2 = cst[:, 896:896 + NIT]
                onesf = cst[:, 384:512]
                iqTv = iqT_d.rearrange("(j two d) t -> two d j t", two=2, d=64)
                qTv = qT_d.rearrange("(h d) t -> d h t", d=64)
                yaTv = yaT_d.rearrange("(h d) t -> d h t", d=64)
                iwv = iw_d.rearrange("b p f -> p b f")
                il_banks = [psums[0], psums[1]]
                p_sc = psums[2]
                p_tr = psums[3]
                lg_banks = [psums[4], psums[5], psums[6]]
                p_acc = psums[7]
                cnt_i = dict(il=0, rl=0, lg=0, pe=0, pm=0, kv=0, hd=0)

                fm_state = {}

                def idx_load(i):
                    e0 = i * TW
                    iq_, iw_ = iq_sb[i % 2], iw_sb[i % 2]
                    kb.DMA(iq_[0:64, :, 0, :], iqTv[0, :, :, e0:e0 + TW], [], [iq_], partial=False)
                    kb.DMA(iq_[64:128, :, 1, :], iqTv[1, :, :, e0:e0 + TW], [], [iq_], partial=True)
                    kb.DMA(iw_[:, :, :], iwv[:, 3 * i:3 * i + 3, :], [], [iw_], partial=False)
                    fm_state[i] = True

                def idx_stream(i, qq0):
                    iq_, iw_, mT = iq_sb[i % 2], iw_sb[i % 2], maskT[i % 2]
                    nsb_t = 34 + 3 * i
                    first_mask = fm_state[i]
                    sc = scs[(3 * i + qq0) % 2]
                    scb = sc.subs
                    for qq in (qq0,):
                        qb = 3 * i + qq
                        nb = 32 + qb
                        L = nb * 128
                        nkt = (L + 511) // 512
                        if nb < nsb_t:
                            kb.G(I("memset", mT[:, nb:nsb_t, qq * 128:(qq + 1) * 128], 1.0), [], [mT], partial=not first_mask)
                            first_mask = False
                            fm_state[i] = False
                        for h in range(8):
                            kb.G(I("tensor_scalar", out=Dg[:, h, :], in0=cst[:, 0:128], scalar1=iw_[:, qq, 8 + h:9 + h], scalar2=1.0,
                                   op0=ALU.mult, op1=ALU.mult), [cst, iw_], [Dg], partial=(h > 0))
                        for kt in range(nkt):
                            w = min(512, L - kt * 512)
                            prev = None
                            for h in range(8):
                                pil = il_banks[cnt_i["il"] % 2]
                                cnt_i["il"] += 1
                                hp = (h % 2) * 64
                                kb.PE(I("matmul", pil[:, :w], lhsT=iq_[:, h // 2, h % 2, qq * 128:(qq + 1) * 128],
                                        rhs=ikT2[:, kt * 512:kt * 512 + w], start=True, stop=True), [iq_, ikT2], [pil])
                                r_ = rl[cnt_i["rl"] % 4]
                                cnt_i["rl"] += 1
                                kb.A(I("activation", out=r_[:, :w], in_=pil[:, :w], func=AF.Relu, scale=iw_[:, qq, h:h + 1]), [pil, iw_], [r_])
                                if prev is not None:
                                    ph, pr = prev
                                    kb.PE(I("matmul", p_sc[:, :w], lhsT=Dg[:, ph, :], rhs=pr[:, :w], start=(ph == 0), stop=False), [Dg, pr], [p_sc])
                                prev = (h, r_)
                            ph, pr = prev
                            kb.PE(I("matmul", p_sc[:, :w], lhsT=Dg[:, ph, :], rhs=pr[:, :w], start=False, stop=True), [Dg, pr], [p_sc])
                            c0 = kt * 512
                            kb.A(I("activation", out=sc[:, c0:c0 + w], in_=p_sc[:, :w], func=AF.Copy), [p_sc], [scb[kt]])
                            kb.V(I("tensor_reduce", out=amx[:, kt:kt + 1], in_=sc[:, c0:c0 + w], op=ALU.max, axis=AX.X, apply_absolute_value=True),
                                 [scb[kt]], [amx], partial=(kt > 0))
                            if kt < 8:
                                kb.V(I("tensor_scalar", out=sc[:, c0:c0 + w], in0=sc[:, c0:c0 + w], scalar1=flags_[:, 1:2], scalar2=None, op0=ALU.add),
                                     [scb[kt], flags_], [scb[kt]], partial=True)
                            yield 3.0
                        kb.V(I("tensor_tensor", out=sc[:, L - 128:L], in0=sc[:, L - 128:L], in1=negtriT, op=ALU.add), [scb[nkt - 1], cst], [scb[nkt - 1]], partial=True)
                        yield ("idx_done", nkt * 8 * 0.6)
                        kb.V(I("tensor_reduce", out=smax[:, :], in_=amx[:, :nkt], op=ALU.max, axis=AX.X), [amx], [smax])
                        kb.V(I("tensor_scalar", out=smax[:, :], in0=smax[:, :], scalar1=1.0, scalar2=None, op0=ALU.add), [smax], [smax])
                        kb.V(I("tensor_scalar", out=lo[:, :], in0=smax[:, :], scalar1=-1.0, scalar2=None, op0=ALU.mult), [smax], [lo])
                        kb.V(I("tensor_scalar", out=wall[:, :], in0=pow2, scalar1=smax[:, 0:1], scalar2=2.0, op0=ALU.mult, op1=ALU.mult),
                             [cst, smax], [wall])
                        for it in range(NIT):
                            kb.V(I("tensor_tensor", out=mid[:, :], in0=lo[:, :], in1=wall[:, it:it + 1], op=ALU.add), [lo, wall], [mid])
                            kb.V(I("tensor_scalar", out=mk[:, :L], in0=sc[:, :L], scalar1=mid[:, 0:1], scalar2=None, op0=ALU.is_ge, op1=ALU.add,
                                   accum_out=cnt[:, 0:1]), scb[:nkt] + [mid], [mk, cnt])
                            kb.V(I("tensor_scalar", out=ge[:, :], in0=cnt[:, :], scalar1=255.5, scalar2=None, op0=ALU.is_ge), [cnt], [ge])
                            kb.V(I("scalar_tensor_tensor", out=lo[:, :], in0=ge[:, :], scalar=wall[:, it:it + 1], in1=lo[:, :], op0=ALU.mult, op1=ALU.add),
                                 [ge, wall, lo], [lo])
                        kb.V(I("tensor_scalar", out=mk[:, :L], in0=sc[:, :L], scalar1=lo[:, 0:1], scalar2=None, op0=ALU.is_lt), scb[:nkt] + [lo], [mk])
                        yield ("bis_done", (NIT + 1) * (L / 960.0 + 1.0))
                        for s0 in range(0, nb, 8):
                            ns = min(8, nb - s0)
                            ptb = p_tr.bf()
                            for j in range(ns):
                                kb.PE(I("transpose", out=ptb[:, j * 128:(j + 1) * 128], in_=mk[:, (s0 + j) * 128:(s0 + j + 1) * 128], identity=sh.ident[:, :]),
                                      [mk, sh.ident], [p_tr])
                            kb.A(I("activation", out=mT[:, s0:s0 + ns, qq * 128:(qq + 1) * 128],
                                   in_=ptb[:, 0:ns * 128].rearrange("p (b t) -> p b t", t=128), func=AF.Copy, saturate=False), [p_tr], [mT], partial=not first_mask)
                            first_mask = False
                            fm_state[i] = False
                            yield 1.2

                def attn_stream(i):
                    e0 = i * TW
                    nsb_t = 34 + 3 * i
                    q_, mT, ya_ = qT_sb[0], maskT[i % 2], ya_sb[0]
                    kb.DMA(q_[0:64, :, :], qTv[:, :, e0:e0 + TW], [], [q_], partial=False)
                    for h in range(8):
                        pend = []
                        its = []
                        for cb0 in range(0, nsb_t, 16):
                            ncb = min(16, nsb_t - cb0)
                            kc_, vc_ = kc[cnt_i["kv"] % 2], vc[cnt_i["kv"] % 2]
                            cnt_i["kv"] += 1
                            kb.DMA(kc_[0:64, :ncb * 128], kT_d[h * 64:(h + 1) * 64, cb0 * 128:(cb0 + ncb) * 128], [], [kc_], partial=False)
                            kb.DMA(vc_[:, :ncb, :], V_d[h, :, cb0:cb0 + ncb, :], [], [vc_], partial=False)
                            for j in range(ncb):
                                sb = cb0 + j
                                plg = lg_banks[cnt_i["lg"] % 3]
                                cnt_i["lg"] += 1
                                kb.PE(I("matmul", plg[:, :TW], lhsT=kc_[:, j * 128:(j + 1) * 128], rhs=q_[:, h, :], start=True, stop=False),
                                      [kc_, q_], [plg])
                                kb.PE(I("matmul", plg[:, :TW], lhsT=negI[:, :], rhs=mT[:, sb, :], start=False, stop=True), [negI, mT], [plg])
                                e_ = pe_[cnt_i["pe"] % 4]
                                cnt_i["pe"] += 1
                                kb.A(I("activation", out=e_[:, :], in_=plg[:, :TW], func=AF.Exp, scale=0.125, bias=negM[:, h:h + 1]), [plg, negM], [e_])
                                pend.append((sb, vc_, j, e_))
                                if len(pend) > 2:
                                    sb2, v2_, j2, m2 = pend.pop(0)
                                    kb.PE(I("matmul", p_acc[0:65, :TW], lhsT=v2_[:, j2, :], rhs=m2[:, :], start=(sb2 == 0), stop=(sb2 == nsb_t - 1)),
                                          [v2_, m2], [p_acc])
                                if sb % 4 == 3:
                                    yield 2.4
                        for sb2, v2_, j2, m2 in pend:
                            kb.PE(I("matmul", p_acc[0:65, :TW], lhsT=v2_[:, j2, :], rhs=m2[:, :], start=(sb2 == 0), stop=(sb2 == nsb_t - 1)),
                                  [v2_, m2], [p_acc])
                        hd = cnt_i["hd"] % 2
                        cnt_i["hd"] += 1
                        o_, rd_, rb_ = o_sb[hd], rden[hd], rb_sb[hd]
                        kb.A(I("activation", out=o_[:, :], in_=p_acc[0:65, :TW], func=AF.Copy), [p_acc], [o_])
                        kb.G(I("tensor_scalar", out=rd_[64:65, :], in0=o_[64:65, :], scalar1=1e-30, scalar2=1.0, op0=ALU.max, op1=ALU.mult), [o_], [rd_])
                        kb.A(I("activation", out=rd_[64:65, :], in_=rd_[64:65, :], func=AF.Ln), [rd_], [rd_])
                        kb.A(I("activation", out=rd_[64:65, :], in_=rd_[64:65, :], func=AF.Exp, scale=-1.0), [rd_], [rd_])
                        kb.PE(I("matmul", p_tr[0:64, :TW], lhsT=onesf[64:65, 0:64], rhs=rd_[64:65, :], start=True, stop=True), [cst, rd_], [p_tr])
                        kb.A(I("activation", out=rb_[:, :], in_=p_tr[0:64, :TW], func=AF.Copy), [p_tr], [rb_])
                        kb.G(I("tensor_tensor", out=ya_[:, h, :], in0=o_[0:64, :], in1=rb_[:, :], op=ALU.mult), [o_, rb_], [ya_], partial=(h > 0))
                        yield 3.0
                    kb.DMA(yaTv[:, :, e0:e0 + TW], ya_[:, :, :], [ya_], [], eng="gpsimd")

                def drain(g):
                    for _ in g:
                        pass

                def run_until(g, tag):
                    for r in g:
                        if isinstance(r, tuple) and r[0] == tag:
                            return r[1]
                    return 0.0

                def fill(ga, t):
                    if ga is None:
                        return
                    while t > 0:
                        try:
                            t -= next(ga)
                        except StopIteration:
                            return

                BISF = float(os.environ.get("BISF", "0.9"))

                def tile_sched(i, ga):
                    idx_load(i)
                    g = [idx_stream(i, qq) for qq in range(3)]
                    run_until(g[0], "idx_done")
                    tb0 = run_until(g[0], "bis_done")
                    ti1 = run_until(g[1], "idx_done")
                    fill(ga, (tb0 - ti1) * BISF)
                    drain(g[0])
                    tb1 = run_until(g[1], "bis_done")
                    ti2 = run_until(g[2], "idx_done")
                    fill(ga, (tb1 - ti2) * BISF)
                    drain(g[1])
                    tb2 = run_until(g[2], "bis_done")
                    fill(ga, tb2 * BISF)
                    drain(g[2])
                    if ga is not None:
                        drain(ga)

                tile_sched(0, None)
                for i in range(P2NT):
                    if i + 1 < P2NT:
                        tile_sched(i + 1, attn_stream(i))
                    else:
                        drain(attn_stream(i))
                P.barrier()

        def phase_3():
            with ExitStack() as pes:
                common_tiles(pes)
                sh.nps = 8
                wo = kb.sb(pes, "wo0", [128, 12, D], BF16)
                xf = [kb.sb(pes, "xf%d" % i, [128, 8, TW], F32) for i in range(2)]
                xo = [kb.sb(pes, "xo%d" % i, [128, 8, TW], F32) for i in range(2)]
                ya = [kb.sb(pes, "yat%d" % i, [128, 4, TW], BF16) for i in range(2)]
                yb = [kb.sb(pes, "ybt%d" % i, [128, 8, TW], BF16) for i in range(2)]
                sv = xview(xT[:, EXT0:S])
                dv = xview(xa_d)
                yav = yaT_d.rearrange("(k p) t -> p k t", p=128)
                ybv = ybT_d.rearrange("(k p) t -> p k t", p=128)
                def p3_pre(i):
                    t0 = i * TW
                    x_, a_, b_ = xf[i % 2], ya[i % 2], yb[i % 2]
                    kb.DMA(x_[:, :, :], sv[:, :, t0:t0 + TW], [], [x_], partial=False)
                    kb.DMA(a_[:, :, :], yav[:, :, t0:t0 + TW], [], [a_], partial=False)
                    kb.DMA(b_[:, :, :], ybv[:, :, t0:t0 + TW], [], [b_], partial=False)

                p3_pre(0)
                p3_pre(1)
                load_weight(kb, sh, wo, l0_out_w, 12, D, None, act_share=True)
                for i in range(NT):
                    t0 = i * TW
                    x_, o_, a_, b_ = xf[i % 2], xo[i % 2], ya[i % 2], yb[i % 2]
                    if i >= 2:
                        p3_pre(i)
                    for n in range(8):
                        ps = next_ps()
                        proj(kb, ps, wo, n * 128, a_, TW, nk=4, k0=0, start=True, stop=False)
                        proj(kb, ps, wo, n * 128, b_, TW, nk=8, k0=4, start=False, stop=True)
                        kb.V(I("tensor_tensor", out=o_[:, n, :], in0=ps[:, :TW], in1=x_[:, n, :], op=ALU.add), [ps, x_], [o_], partial=(n > 0))
                    kb.DMA(dv[:, :, t0:t0 + TW], o_[:, :, :], [o_], [], eng="gpsimd")
                P.barrier()

        sh.outbuf = T(None, "outbuf")
        P.barrier()
        if debug == "p1a":
            phase_1a()
        if debug is None:
            phase_1a()
            phase_1b()
            phase_2()
            phase_3()
            phase_ffn_a(xa_d, ffn_w[0][0], ffn_w[0][1], "l0_ffn_norm")
            phase_ffn_b(xa_d, ffn_w[0][2], xb_d)
            phase_l1_mixer(xb_d, xc_d)
            phase_ffn_a(xc_d, ffn_w[1][0], ffn_w[1][1], "l1_ffn_norm")
            phase_ffn_b(xc_d, ffn_w[1][2], None, final_gain="final_norm")
        if debug == "p2":
            phase_1a()
            phase_2()
        if debug == "p1b":
            phase_1b()
        if debug == "l1only":
            xsrc = xT[:, EXT0:S]
            phase_l1_mixer(xsrc, xc_d)
            phase_ffn_a(xc_d, ffn_w[1][0], ffn_w[1][1], "l1_ffn_norm")
            phase_ffn_b(xc_d, ffn_w[1][2], None, final_gain="final_norm")
        P.barrier()
        print("ops", P.nops, "waits", P.nwaits, {k: v for k, v in P.cnt.items()})
        P.emit()
    return nc


def host_consts():
    c = np.zeros((128, 1024), np.float32)
    c[:, 0:128] = np.eye(128, dtype=np.float32)
    s = np.arange(128)
    c[:, 128:256] = (s[:, None] <= s[None, :]).astype(np.float32)
    c[:, 256:384] = np.where(s[:, None] <= s[None, :], 0.0, -30000.0)
    c[:, 384:512] = 1.0
    c[:, 768:896] = np.where(s[None, :] <= s[:, None], 0.0, NEG)
    c[:, 896:896 + 32] = (2.0 ** -(np.arange(32) + 1.0))[None, :]
    c[:64, 512:640] = 1.0
    c[64:, 640:768] = 1.0
    return c


def pp(v):
    return np.ascontiguousarray(v.reshape(-1, 128).T)


def host_vecs(inp):
    v = np.zeros((128, 64), np.float32)
    v[:, 0:8] = pp(inp["l0_attn_norm"])
    v[:, 8:16] = pp(inp["l0_ffn_norm"])
    v[:, 16:24] = pp(inp["l1_conv_norm"])
    v[:, 24:32] = pp(inp["l1_ffn_norm"])
    v[:, 32:40] = pp(inp["final_norm"])
    cw = inp["l1_conv_w"]
    v[:, 40:64] = cw.reshape(3, 8, 128).transpose(2, 1, 0).reshape(128, 24)
    return v


def host_vecs2(inp):
    v = np.zeros((128, 128), np.float32)
    v[:, 0:2] = pp(inp["l0_kv_norm"])
    v[:, 2:50] = inp["l0_conv_w"].reshape(4, 12, 128).transpose(2, 1, 0).reshape(128, 48)
    v[:, 50:62] = pp(inp["l0_conv_b"])
    v[:, 62:70] = pp(inp["l0_ssm_norm"])
    v[:, 70:86] = inp["l0_dt_bias"][None, :]
    v[:, 86:102] = inp["l0_A_log"][None, :]
    v[:, 102:118] = inp["l0_D"][None, :]
    return v


def make_in_maps(inp):
    x = inp["x"]
    maps = []
    consts = host_consts()
    vecs = host_vecs(inp)
    vecs2 = host_vecs2(inp)
    wukT = np.ascontiguousarray(inp["l0_w_uk"].transpose(2, 0, 1).reshape(256, 512))
    wuv = np.ascontiguousarray(inp["l0_w_uv"].transpose(1, 0, 2).reshape(256, 512))
    for c in range(8):
        b, hf = c // 2, c % 2
        xt = np.zeros((D, S), np.float32)
        if hf == 1:
            xt[:, :] = x[b].T
        else:
            xt[:, HALF:] = x[b, :HALF].T
        fl = np.zeros((128, 2), np.float32)
        fl[:, 0] = float(hf)
        fl[:, 1] = (float(hf) - 1.0) * 1.0e30
        m = {"xT": xt, "consts": consts, "flags": fl, "vecs": vecs, "vecs2": vecs2,
             "l0_in_w": inp["l0_in_w"], "l0_out_w": inp["l0_out_w"], "wukT": wukT, "wuv": wuv,
             "l1_in_w": inp["l1_in_w"], "l1_out_w": inp["l1_out_w"]}
        for nm in ("l0_w_gate", "l0_w_up", "l0_w_down", "l1_w_gate", "l1_w_up", "l1_w_down"):
            m[nm] = inp[nm]
        maps.append(m)
    return maps


def kernel(**inputs):
    inp = {k: np.asarray(v) for k, v in inputs.items()}
    nc = build_program()
    maps = make_in_maps(inp)
    res = run_bass_kernel_spmd(nc, maps, core_ids=list(range(8)))
    out = np.zeros((4, S, D), np.float32)
    for c in range(8):
        b, hf = c // 2, c % 2
        out[b, hf * HALF:(hf + 1) * HALF, :] = res.results[c]["yT"].T
    return out
```
